# Optimizing a Trainium2 kernel written in Bass

```python
import jax, jax.numpy as jnp
from jax import lax
import numpy as np

D_MODEL = 2048
BATCH = 4
SEQ = 8192
DEPTH = 1
DEC_BATCH = 8
DEC_SEQ = 64
PAST_LEN = 1024

CHUNK = 64
WINDOW = 128
WIN_CHUNKS = WINDOW // CHUNK
SWA_HEADS = 16
SWA_KV_HEADS = 4
SWA_GROUP = SWA_HEADS // SWA_KV_HEADS
SWA_HEAD_DIM = 64
RET_HEADS = 8
RET_QK_DIM = 128
RET_V_DIM = 128
RET_ROPE_BASE = 10000.0
N_MEM = 256
MEM_HEADS = 4
MEM_HEAD_DIM = 256
D_FF = 5632
CONV_W = 3
N_BRANCH = 3
EPS = 1e-6
NEG = -1e30

SWA_Q = SWA_HEADS * SWA_HEAD_DIM
SWA_KV = SWA_KV_HEADS * SWA_HEAD_DIM
RET_QK = RET_HEADS * RET_QK_DIM
RET_V = RET_HEADS * RET_V_DIM
MEM_W = MEM_HEADS * MEM_HEAD_DIM
IN_SPLITS = (SWA_Q, SWA_KV, SWA_KV, RET_QK, RET_QK, RET_V, RET_V, MEM_W, N_BRANCH * D_MODEL)
IN_COLS = sum(IN_SPLITS)
MIX_W = SWA_Q + RET_V + MEM_W

kernel_name = "hybrid_streaming_swa_retention_step"


def rmsnorm(x, g):
    xf = x.astype(jnp.float32)
    y = xf * lax.rsqrt(jnp.mean(xf * xf, axis=-1, keepdims=True) + EPS)
    return (y * g.astype(jnp.float32)).astype(x.dtype)


def rotary(x, pos):
    half = x.shape[-1] // 2
    inv_freq = 1.0 / (RET_ROPE_BASE ** jnp.linspace(0.0, 1.0, half, dtype=jnp.float32))
    ang = pos.astype(jnp.float32)[:, None] * inv_freq[None, :]
    cos = jnp.cos(ang)[None, :, None, :]
    sin = jnp.sin(ang)[None, :, None, :]
    xf = x.astype(jnp.float32)
    x1, x2 = xf[..., :half], xf[..., half:]
    return jnp.concatenate([x1 * cos - x2 * sin, x1 * sin + x2 * cos], axis=-1).astype(x.dtype)


def sink_softmax(s, sink):
    sk = sink.astype(jnp.float32)[:, :, None, None]
    m = jnp.maximum(jnp.max(s, axis=-1, keepdims=True), sk)
    p = jnp.exp(s - m)
    return p / (jnp.sum(p, axis=-1, keepdims=True) + jnp.exp(sk - m))


def swa_banded(q, k, v, sink):
    B, S = q.shape[:2]
    nc = S // CHUNK
    qb = q.reshape(B, nc, CHUNK, SWA_KV_HEADS, SWA_GROUP, SWA_HEAD_DIM)

    def band(x):
        xb = jnp.pad(x.reshape(B, nc, CHUNK, SWA_KV_HEADS, SWA_HEAD_DIM),
                     ((0, 0), (WIN_CHUNKS, 0), (0, 0), (0, 0), (0, 0)))
        return jnp.concatenate([xb[:, j:j + nc] for j in range(WIN_CHUNKS + 1)], axis=2)

    kb, vb = band(k), band(v)
    chunk_id = jnp.arange(nc)[:, None] - WIN_CHUNKS + jnp.arange(WIN_CHUNKS + 1)[None, :]
    valid = jnp.repeat(chunk_id >= 0, CHUNK, axis=1)
    s = jnp.einsum('bnqhgd,bnkhd->bnhgqk', qb, kb).astype(jnp.float32) * (SWA_HEAD_DIM ** -0.5)
    s = jnp.where(valid[None, :, None, None, None, :], s, NEG)
    p = sink_softmax(s, sink.reshape(SWA_KV_HEADS, SWA_GROUP)).astype(v.dtype)
    o = jnp.einsum('bnhgqk,bnkhd->bnqhgd', p, vb)
    return o.reshape(B, S, SWA_Q)


def swa_step(q, k_all, v_all, sink):
    B, T = q.shape[:2]
    qg = q.reshape(B, T, SWA_KV_HEADS, SWA_GROUP, SWA_HEAD_DIM)
    s = jnp.einsum('bqhgd,bkhd->bhgqk', qg, k_all).astype(jnp.float32) * (SWA_HEAD_DIM ** -0.5)
    p = sink_softmax(s, sink.reshape(SWA_KV_HEADS, SWA_GROUP)).astype(v_all.dtype)
    o = jnp.einsum('bhgqk,bkhd->bqhgd', p, v_all)
    return o.reshape(B, T, SWA_Q)


def ret_log_decay():
    return jnp.log1p(-jnp.exp2(-5.0 - jnp.arange(RET_HEADS, dtype=jnp.float32)))


def retention_chunk(q, k, v, state):
    C = q.shape[2]
    log_g = ret_log_decay()
    n = jnp.arange(C, dtype=jnp.float32)
    diff = n[:, None] - n[None, :]
    decay = jnp.where(diff >= 0, jnp.exp(log_g[:, None, None] * jnp.maximum(diff, 0.0)), 0.0)
    qf, kf, vf, sf = (a.astype(jnp.float32) for a in (q, k, v, state))
    inner = jnp.einsum('bhnd,bhmd->bhnm', qf, kf) * decay
    o = jnp.einsum('bhnm,bhme->bhne', inner, vf)
    o = o + jnp.einsum('bhnd,bhde->bhne', qf, sf) * jnp.exp(log_g[:, None] * (n + 1.0))[:, :, None]
    zeta = jnp.exp(log_g[:, None] * (C - 1.0 - n))
    new_state = (jnp.exp(log_g * C)[:, None, None] * sf
                 + jnp.einsum('bhmd,bhme->bhde', kf * zeta[:, :, None], vf))
    return o, new_state


def retention_chunked(q, k, v):
    B, H, S, _ = q.shape
    nc = S // CHUNK

    def blocks(a):
        return jnp.moveaxis(a.reshape(B, H, nc, CHUNK, a.shape[-1]), 2, 0)

    def step(state, qkv):
        o, state = retention_chunk(qkv[0], qkv[1], qkv[2], state)
        return state, o

    s0 = jnp.zeros((B, H, RET_QK_DIM, RET_V_DIM), jnp.float32)
    s_final, o = lax.scan(step, s0, (blocks(q), blocks(k), blocks(v)))
    return jnp.moveaxis(o, 0, 2).reshape(B, H, S, RET_V_DIM), s_final


def retention_out(o, gate):
    B, H, T, _ = o.shape
    o = o * lax.rsqrt(jnp.mean(o * o, axis=-1, keepdims=True) + EPS)
    o = jnp.transpose(o, (0, 2, 1, 3)).reshape(B, T, RET_V)
    return (jax.nn.silu(gate.astype(jnp.float32)) * o).astype(gate.dtype)


def memory_kv(mem, g_mem, w_mem_kv):
    B, M, _ = mem.shape
    k, v = jnp.split(rmsnorm(mem, g_mem) @ w_mem_kv, 2, axis=-1)
    return (k.reshape(B, M, MEM_HEADS, MEM_HEAD_DIM), v.reshape(B, M, MEM_HEADS, MEM_HEAD_DIM))


def memory_attend(q, mk, mv):
    B, T = q.shape[:2]
    qh = q.reshape(B, T, MEM_HEADS, MEM_HEAD_DIM)
    s = jnp.einsum('bthd,bmhd->bhtm', qh, mk).astype(jnp.float32) * (MEM_HEAD_DIM ** -0.5)
    p = jax.nn.softmax(s, axis=-1).astype(mv.dtype)
    return jnp.einsum('bhtm,bmhd->bthd', p, mv).reshape(B, T, MEM_W)


def conv_ffn(u, conv_buf, w_up, w_conv, b_conv, w_down):
    T = u.shape[1]
    a = u @ w_up
    ext = jnp.concatenate([conv_buf.astype(a.dtype), a], axis=1)
    c = b_conv + sum(ext[:, j:j + T] * w_conv[j] for j in range(CONV_W))
    g, val = jnp.split(c, 2, axis=-1)
    return (jax.nn.silu(g) * val) @ w_down, ext[:, T:]


def run_layer(h, pos, mem_k, mem_v, swa_cache, ret_state, conv_buf,
              g_mix, w_in, b_gate, sink, w_br, w_o, g_ffn, w_up, w_conv, b_conv, w_down):
    B, T, _ = h.shape
    u = rmsnorm(h, g_mix)
    split_pts = np.cumsum(IN_SPLITS)[:-1].tolist()
    qa, ka, va, qr, kr, vr, gr, qm, gl = jnp.split(u @ w_in, split_pts, axis=-1)
    qa = qa.reshape(B, T, SWA_HEADS, SWA_HEAD_DIM)
    ka = ka.reshape(B, T, SWA_KV_HEADS, SWA_HEAD_DIM)
    va = va.reshape(B, T, SWA_KV_HEADS, SWA_HEAD_DIM)
    if swa_cache is None:
        o_swa = swa_banded(qa, ka, va, sink)
        new_k, new_v = ka[:, -WINDOW:], va[:, -WINDOW:]
    else:
        k_all = jnp.concatenate([swa_cache[0].astype(ka.dtype), ka], axis=1)
        v_all = jnp.concatenate([swa_cache[1].astype(va.dtype), va], axis=1)
        o_swa = swa_step(qa, k_all, v_all, sink)
        n_keep = swa_cache[0].shape[1]
        new_k, new_v = k_all[:, -n_keep:], v_all[:, -n_keep:]
    qr = rotary(qr.reshape(B, T, RET_HEADS, RET_QK_DIM), pos)
    kr = rotary(kr.reshape(B, T, RET_HEADS, RET_QK_DIM), pos) * (RET_QK_DIM ** -0.5)
    vr = vr.reshape(B, T, RET_HEADS, RET_V_DIM)
    qt, kt, vt = (jnp.transpose(a, (0, 2, 1, 3)) for a in (qr, kr, vr))
    if ret_state is None:
        o_r, s_new = retention_chunked(qt, kt, vt)
    else:
        o_r, s_new = retention_chunk(qt, kt, vt, ret_state)
    o_ret = retention_out(o_r, gr)
    o_mem = memory_attend(qm, mem_k.astype(h.dtype), mem_v.astype(h.dtype))
    gates = jax.nn.sigmoid((gl + b_gate).astype(jnp.float32)).astype(h.dtype).reshape(B, T, N_BRANCH, D_MODEL)
    merged = (gates[:, :, 0] * (o_swa @ w_br[:SWA_Q])
              + gates[:, :, 1] * (o_ret @ w_br[SWA_Q:SWA_Q + RET_V])
              + gates[:, :, 2] * (o_mem @ w_br[SWA_Q + RET_V:]))
    h = h + merged @ w_o
    f, new_buf = conv_ffn(rmsnorm(h, g_ffn), conv_buf, w_up, w_conv, b_conv, w_down)
    return h + f, new_k, new_v, s_new.astype(h.dtype), new_buf


def setup_inputs(seed: int = 0) -> dict:
    key = jax.random.key(seed)
    ks = jax.random.split(key, 24)
    f32 = jnp.float32

    def nrm(k, shape, scale):
        return jax.random.normal(k, shape, f32) * scale

    swa_len = min(WINDOW, PAST_LEN)
    return {
        "x_prompt": nrm(ks[0], (BATCH, SEQ, D_MODEL), 1.0),
        "x_sample": nrm(ks[1], (DEC_BATCH, DEC_SEQ, D_MODEL), 1.0),
        "mem_prompt": nrm(ks[2], (BATCH, N_MEM, D_MODEL), 1.0),
        "cache_swa_k": nrm(ks[3], (DEPTH, DEC_BATCH, swa_len, SWA_KV_HEADS, SWA_HEAD_DIM), 1.0),
        "cache_swa_v": nrm(ks[4], (DEPTH, DEC_BATCH, swa_len, SWA_KV_HEADS, SWA_HEAD_DIM), 1.0),
        "state_ret": nrm(ks[5], (DEPTH, DEC_BATCH, RET_HEADS, RET_QK_DIM, RET_V_DIM), 0.5),
        "state_ffn_conv": nrm(ks[6], (DEPTH, DEC_BATCH, CONV_W - 1, 2 * D_FF), 1.0),
        "cache_mem_k": nrm(ks[7], (DEPTH, DEC_BATCH, N_MEM, MEM_HEADS, MEM_HEAD_DIM), 1.0),
        "cache_mem_v": nrm(ks[8], (DEPTH, DEC_BATCH, N_MEM, MEM_HEADS, MEM_HEAD_DIM), 1.0),
        "g_mix": 1.0 + nrm(ks[9], (DEPTH, D_MODEL), 0.02),
        "w_in": nrm(ks[10], (DEPTH, D_MODEL, IN_COLS), D_MODEL ** -0.5),
        "b_gate": nrm(ks[11], (DEPTH, N_BRANCH * D_MODEL), 0.02),
        "sink": nrm(ks[12], (DEPTH, SWA_HEADS), 0.5),
        "w_br": nrm(ks[13], (DEPTH, MIX_W, D_MODEL), SWA_Q ** -0.5),
        "w_o": nrm(ks[14], (DEPTH, D_MODEL, D_MODEL), D_MODEL ** -0.5),
        "g_mem": 1.0 + nrm(ks[15], (DEPTH, D_MODEL), 0.02),
        "w_mem_kv": nrm(ks[16], (DEPTH, D_MODEL, 2 * MEM_W), D_MODEL ** -0.5),
        "g_ffn": 1.0 + nrm(ks[17], (DEPTH, D_MODEL), 0.02),
        "w_up": nrm(ks[18], (DEPTH, D_MODEL, 2 * D_FF), D_MODEL ** -0.5),
        "w_conv": nrm(ks[19], (DEPTH, CONV_W, 2 * D_FF), CONV_W ** -0.5),
        "b_conv": nrm(ks[20], (DEPTH, 2 * D_FF), 0.02),
        "w_down": nrm(ks[21], (DEPTH, D_FF, D_MODEL), D_FF ** -0.5),
        "g_final": 1.0 + nrm(ks[22], (D_MODEL,), 0.02),
    }


def reference(x_prompt, x_sample, mem_prompt, cache_swa_k, cache_swa_v, state_ret, state_ffn_conv,
              cache_mem_k, cache_mem_v, g_mix, w_in, b_gate, sink, w_br, w_o, g_mem, w_mem_kv,
              g_ffn, w_up, w_conv, b_conv, w_down, g_final):
    Bp, S, _ = x_prompt.shape
    T = x_sample.shape[1]
    pos_p = jnp.arange(S)
    pos_s = PAST_LEN + jnp.arange(T)
    hp, hs = x_prompt, x_sample
    kp_l, vp_l, sp_l, cp_l, mk_l, mv_l = [], [], [], [], [], []
    ks_l, vs_l, ss_l, cs_l = [], [], [], []
    for l in range(DEPTH):
        w = (g_mix[l], w_in[l], b_gate[l], sink[l], w_br[l], w_o[l],
             g_ffn[l], w_up[l], w_conv[l], b_conv[l], w_down[l])
        mk, mv = memory_kv(mem_prompt, g_mem[l], w_mem_kv[l])
        zero_buf = jnp.zeros((Bp, CONV_W - 1, 2 * D_FF), hp.dtype)
        hp, kp, vp, sp, cp = run_layer(hp, pos_p, mk, mv, None, None, zero_buf, *w)
        hs, ksn, vsn, ssn, csn = run_layer(hs, pos_s, cache_mem_k[l], cache_mem_v[l],
                                            (cache_swa_k[l], cache_swa_v[l]), state_ret[l],
                                            state_ffn_conv[l], *w)
        kp_l.append(kp); vp_l.append(vp); sp_l.append(sp); cp_l.append(cp); mk_l.append(mk); mv_l.append(mv)
        ks_l.append(ksn); vs_l.append(vsn); ss_l.append(ssn); cs_l.append(csn)
    y_prompt = rmsnorm(hp, g_final)
    y_sample = rmsnorm(hs, g_final)
    return (y_prompt, y_sample,
            jnp.stack(kp_l), jnp.stack(vp_l), jnp.stack(sp_l), jnp.stack(cp_l),
            jnp.stack(mk_l), jnp.stack(mv_l),
            jnp.stack(ks_l), jnp.stack(vs_l), jnp.stack(ss_l), jnp.stack(cs_l))
```

```python
import numpy as np
from contextlib import ExitStack
import concourse.bass as bass
import concourse.mybir as mybir
from concourse.bass_utils import run_bass_kernel_spmd

F32 = mybir.dt.float32
BF16 = mybir.dt.bfloat16
AF = mybir.ActivationFunctionType
ALU = mybir.AluOpType
AX = mybir.AxisListType

NDMASEM = 12
NDMAQ = {'sp': 8, 'pool': 4, 'act': 4}
D = 2048
KC = 16
NCORE = 8
HALF = 4096
EPS = 1e-6
NEGM = -30000.0
DFF = 5632


class Prog:
    def __init__(self):
        self.ops = []
        self.last_w = {}
        self.rd_eng = {}
        self.rd_dma = {}

    def op(self, eng, fn, r=(), w=(), dma=False):
        idx = len(self.ops)
        deps = set()
        pr = [k for k in r if isinstance(k, tuple) and k[0] == 'ps']
        if pr:
            r = [k for k in r if not (isinstance(k, tuple) and k[0] == 'ps')]
            w = list(w) + [k for k in pr if k not in w]
        for k in r:
            d = self.last_w.get(k)
            if d is not None:
                deps.add(d)
        for k in w:
            d = self.last_w.get(k)
            if d is not None:
                deps.add(d)
            for x in self.rd_eng.get(k, {}).values():
                deps.add(x)
            for x in self.rd_dma.get(k, ()):
                deps.add(x)
        for k in r:
            if dma:
                self.rd_dma.setdefault(k, []).append(idx)
            else:
                self.rd_eng.setdefault(k, {})[eng] = idx
        for k in w:
            self.last_w[k] = idx
            self.rd_eng[k] = {}
            self.rd_dma[k] = []
        deps.discard(idx)
        self.ops.append(dict(eng=eng, fn=fn, deps=deps, dma=dma))
        return idx

    def emit(self, nc, sems):
        ops = self.ops
        for o in ops:
            o['sig'] = o['dma']
        for o in ops:
            for d in o['deps']:
                od = ops[d]
                if od['eng'] == 'pe' and o['eng'] == 'pe' and not od['dma'] and not o['dma']:
                    continue
                od['sig'] = True
        seq = {'pe': 0, 'act': 0, 'dve': 0, 'pool': 0}
        dcount = {}
        dma_n = {'sp': 0, 'pool': 0, 'act': 0}
        for o in ops:
            if o['dma']:
                q = o['eng']
                j = dma_n[q] % NDMAQ[q]
                dma_n[q] += 1
                key = (q, j)
                dcount[key] = dcount.get(key, 0) + 1
                o['sem'] = key
                o['val'] = 16 * dcount[key]
            elif o['sig']:
                seq[o['eng']] += 1
                o['sem'] = o['eng']
                o['val'] = seq[o['eng']]
        streams = {e: [] for e in ('pe', 'act', 'dve', 'pool', 'sp')}
        for i, o in enumerate(ops):
            streams[o['eng']].append(i)
        final_dma = dict(dcount)
        self.stats = dict(seq=dict(seq), dma=dict(dma_n), nops=len(ops))
        print('PROG stats', self.stats, flush=True)

        def run_stream(eng_name, eng):
            waited = {}
            for i in streams[eng_name]:
                o = ops[i]
                need = {}
                for d in o['deps']:
                    od = ops[d]
                    if (od['eng'] == 'pe' and eng_name == 'pe'
                            and not od['dma'] and not o['dma']):
                        continue
                    s, v = od['sem'], od['val']
                    if need.get(s, 0) < v:
                        need[s] = v
                if o['dma'] and o['val'] > 16:
                    s = o['sem']
                    if need.get(s, 0) < o['val'] - 16:
                        need[s] = o['val'] - 16
                for s, v in need.items():
                    if waited.get(s, 0) < v:
                        eng.wait_ge(sems[s], v)
                        waited[s] = v
                ins = o['fn'](eng)
                if o['dma']:
                    ins.then_inc(sems[o['sem']], 16)
                elif o['sig']:
                    ins.then_inc(sems[o['sem']], 1)
            if eng_name == 'sp':
                for s, c in final_dma.items():
                    eng.wait_ge(sems[s], 16 * c)

        with nc.Block() as block:
            @block.sync
            def _(e):
                run_stream('sp', e)

            @block.tensor
            def _(e):
                run_stream('pe', e)

            @block.scalar
            def _(e):
                run_stream('act', e)

            @block.vector
            def _(e):
                run_stream('dve', e)

            @block.gpsimd
            def _(e):
                run_stream('pool', e)


def build_program():
    nc = bass.Bass("TRN2", target_bir_lowering=False)
    P = Prog()

    def din(name, shape):
        return nc.dram_tensor(name, list(shape), F32, kind="ExternalInput").ap()

    def dout(name, shape):
        return nc.dram_tensor(name, list(shape), F32, kind="ExternalOutput").ap()

    def dscr(name, shape):
        return nc.dram_tensor(name, list(shape), BF16, kind="Internal").ap()

    xp = din("xp", [HALF, D]); xpre = din("xpre", [HALF, D]); xs = din("xs", [64, D])
    mem = din("mem", [256, D])
    ck = din("ck", [128, 256]); cv = din("cv", [128, 256])
    sret = din("sret", [128, 1024]); sconv = din("sconv", [128, 176])
    cmk = din("cmk", [256, 1024]); cmv = din("cmv", [256, 1024])
    w_in = din("w_in", [D, 12800]); w_br = din("w_br", [3072, D]); w_o = din("w_o", [D, D])
    w_mkv = din("w_mkv", [D, D]); w_up = din("w_up", [D, 2 * DFF]); w_dn = din("w_dn", [DFF, D])
    gmixT_d = din("gmixT", [128, 16]); gmemT_d = din("gmemT", [128, 16]); gffnT_d = din("gffnT", [128, 16])
    bgT_d = din("bgT", [128, 48]); wcT_d = din("wcT", [128, 264]); bcT_d = din("bcT", [128, 88])
    sinkP_d = din("sinkP", [16]); gfin_d = din("gfin", [D])
    cs_pre = din("cs_pre", [HALF, 128]); cs_main = din("cs_main", [HALF, 128]); cs_s = din("cs_s", [64, 128])
    dq_d = din("dq", [128, 8]); dk_d = din("dk", [128, 8]); g128_d = din("g128", [8]); g64_d = din("g64", [8])
    caus_d = din("caus", [128, 128]); sel_d = din("sel", [2, 128])
    mstd_d = din("mstd", [2, 512]); m0_d = din("m0", [2, 512]); hflag_d = din("hflag", [128, 1])

    y_o = dout("y", [HALF, D]); ys_o = dout("ys", [64, D])
    kp_o = dout("kp", [128, 256]); vp_o = dout("vp", [128, 256])
    sp_o = dout("sp", [128, 1024]); cp_o = dout("cp", [128, 176])
    mk_o = dout("mk", [256, 1024]); mv_o = dout("mv", [256, 1024])
    ks_o = dout("ks", [128, 256]); vs_o = dout("vs", [128, 256])
    ss_o = dout("ss", [128, 1024]); cs_o = dout("cs", [128, 176])

    s_win = dscr("s_win", [25, 128, 8192]); s_mkv = dscr("s_mkv", [4, 128, 8192])
    s_wbr = dscr("s_wbr", [12, 128, 4096]); s_wo = dscr("s_wo", [4, 128, 8192])
    s_wup = dscr("s_wup", [22, 128, 8192]); s_wdn = dscr("s_wdn", [16, 128, 5632])

    with ExitStack() as es:
        def sb(name, shape, dt):
            return es.enter_context(nc.sbuf_tensor(name, list(shape), dt))

        PS = es.enter_context(nc.psum_tensor("PS", [128, 4096], F32))
        PSB = PS[:, :].bitcast(BF16)

        sems = {}
        for n in ('pe', 'act', 'dve', 'pool'):
            sems[n] = es.enter_context(nc.semaphore("s_" + n))
        for q in ('sp', 'pool'):
            for i in range(NDMASEM):
                sems[(q, i)] = es.enter_context(nc.semaphore("d_%s%d" % (q, i)))

        Xb = sb("Xb", [128, 4, D], F32)
        uTb = sb("uTb", [128, KC, 512], BF16)
        Wp = [sb("Wp%d" % i, [128, 8192], BF16) for i in range(3)]
        Ub = sb("Ub", [128, D], BF16)
        Xn = sb("Xn", [128, D], F32)
        ident = sb("ident", [128, 128], BF16)
        identf = sb("identf", [128, 128], F32)
        gmixT = sb("gmixT_s", [128, 16], F32); gmemT = sb("gmemT_s", [128, 16], F32)
        gffnT = sb("gffnT_s", [128, 16], F32)
        bgT = sb("bgT_s", [128, 48], F32); wcT = sb("wcT_s", [128, 88, 3], F32); bcT = sb("bcT_s", [128, 88], F32)
        sinkb = sb("sinkb", [128, 16], F32)
        dq = sb("dq_s", [128, 8], F32); dk = sb("dk_s", [128, 8], F32)
        g128 = sb("g128_s", [128, 8], F32); g64 = sb("g64_s", [128, 8], F32)
        caus = sb("caus_s", [128, 128], F32)
        mskf = sb("mskf", [128, 1152], F32)
        mskb = sb("mskb", [128, 1152], BF16)
        sel = mskb[:, 0:128]; mstd = mskb[:, 128:640]; m0 = mskb[:, 640:1152]
        ksT = sb("ksT", [128, 2, 640], BF16); vstok = sb("vstok", [128, 5, 256], BF16)
        mkT = sb("mkT", [128, 8, 256], BF16); mvtok = sb("mvtok", [128, 2, 1024], BF16)
        Sf = sb("Sf", [128, 8, 128], F32); Sbb = sb("Sbb", [128, 8, 128], BF16)
        carry = sb("carry", [128, 88, 2], F32)
        cst = [sb("cst%d" % i, [128, 128], F32) for i in range(2)]
        st = sb("stats", [128, 96], F32)
        stp = sb("stats2", [128, 2, 48], F32)
        rinv2 = sb("rinv2", [128, 2, 16], F32)
        fdummy = sb("fdummy", [128, 2], F32)
        hflag = sb("hflag_s", [128, 1], F32)
        ss_ = st[:, 0:1]; std_ = st[:, 1:2]; rstd_ = st[:, 2:3]
        mx8 = st[:, 8:16]; nmx8 = st[:, 16:24]; rs8 = st[:, 24:32]; es8 = st[:, 32:40]
        rinvN = st[:, 40:56]; ssq8 = st[:, 56:64]; std8 = st[:, 64:72]; rstd8 = st[:, 72:80]; den8 = st[:, 80:88]

        OVN = 34 * 1024
        OV = sb("OV", [128, OVN], BF16)
        ovpos = [0]

        def carve(nelem_bf16):
            a = ovpos[0]
            ovpos[0] += nelem_bf16
            assert ovpos[0] <= OVN, ovpos[0]
            return OV[:, a:a + nelem_bf16]

        QT = carve(4096).rearrange("p (a b) -> p a b", b=512)
        obT = carve(4096).rearrange("p (a b) -> p a b", b=512)
        mT = carve(8192).rearrange("p (a b) -> p a b", b=512)
        Pb = carve(2048).rearrange("p (a b) -> p a b", b=256)
        PTs = carve(2048).rearrange("p (a b) -> p a b", b=128)
        otok = carve(1024)
        otok_b = carve(1024)
        otok2 = [otok, otok_b]
        qtok_c = carve(512).rearrange("p (a b) -> p a b", b=128)
        qtok_d = carve(512).rearrange("p (a b) -> p a b", b=128)
        rt = [carve(512).bitcast(F32).rearrange("p (a b) -> p a b", b=64) for _ in range(4)]
        qf = carve(1024).bitcast(F32).rearrange("p (h t d) -> p h t d", t=2, d=64)
        qtok = carve(512).rearrange("p (a b) -> p a b", b=128)
        qtok_b = carve(512).rearrange("p (a b) -> p a b", b=128)
        qtok2 = [qtok, qtok_b, qtok_c, qtok_d]
        gsb = [carve(512) for _ in range(2)]
        ytmp = [carve(512) for _ in range(2)]
        kvst = [carve(512).bitcast(F32) for _ in range(2)]
        mixer_end = ovpos[0]
        ovpos[0] = 0
        hT = carve(44 * 512).rearrange("p (a b) -> p a b", b=512)
        abuf = [[carve(1040).bitcast(F32) for _ in range(2)] for _ in range(2)]
        ctmp = [carve(1024).bitcast(F32) for _ in range(2)]
        sgt = carve(1024).bitcast(F32)
        gfin = carve(4096).bitcast(F32)
        ffn_end = ovpos[0]
        XbB = Xb[:, :, :].rearrange("p a b -> p (a b)").bitcast(BF16)
        krT = XbB[:, 0:4096].rearrange("p (a b) -> p a b", b=512)
        vr = XbB[:, 4096:8192].rearrange("p (a b) -> p a b", b=1024)
        gr = XbB[:, 8192:12288].rearrange("p (a b) -> p a b", b=1024)
        kt = XbB[:, 12288:16384].rearrange("p (a b) -> p a b", b=1024)

        MIXK = [('Pb', 0), ('Pb', 1), ('PTs', 0), ('PTs', 1), ('AT', 0), ('AT', 1), 'QT', 'obT', 'mT', 'Pb', 'PTs', 'otok', 'rt', 'qf', 'qtok', 'qtok0', 'qtok1', 'qtok2', 'qtok3', 'otok0', 'otok1', 'gsb0', 'gsb1', 'ytmp0', 'ytmp1',
                'kvst0', 'kvst1']
        FFNK = ['hT', 'abuf00', 'abuf01', 'abuf10', 'abuf11', 'ctmp0', 'ctmp1', 'sgt', 'gfin']
        XK = [('X', s) for s in range(4)]
        RETK = ['krT', 'vr', 'gr'] + [('kt', s_, hq_) for s_ in range(4) for hq_ in range(2)]

        def fence(old, new, eng='pool'):
            P.op(eng, lambda e: e.memset(fdummy[:, 0:1], 0.0), r=[], w=['fdummy'] + list(old) + list(new))

        def bank(i):
            return PS[:, i * 512:(i + 1) * 512]

        def tbank(i):
            return PSB[:, i * 1024:(i + 1) * 1024].rearrange("p (a b) -> p a b", b=128)

        def pk(i):
            return ('ps', i)

        rot = {'f': 0, 't': 0, 'w': 0, 'cs': 0, 'q': 0}

        def fb():
            i = rot['f'] % 6
            rot['f'] += 1
            return i

        def tb():
            i = 6 + rot['t'] % 2
            rot['t'] += 1
            return i

        def ld(dst, src, key, q='sp'):
            P.op(q, lambda e: e.dma_start(out=dst, in_=src), w=[key], dma=True)

        P.op('pool', lambda e: e.memset(identf[:], 1.0), w=['identf'])
        P.op('pool', lambda e: e.affine_select(out=identf[:], in_=identf[:], pattern=[[-1, 128]],
                                               compare_op=ALU.is_equal, fill=0.0, base=0,
                                               channel_multiplier=1), r=['identf'], w=['identf'])
        P.op('dve', lambda e: e.tensor_copy(out=ident[:], in_=identf[:]), r=['identf'], w=['ident'])
        ld(gmixT[:], gmixT_d, 'gmixT'); ld(gmemT[:], gmemT_d, 'gmemT'); ld(gffnT[:], gffnT_d, 'gffnT')
        ld(bgT[:], bgT_d, 'bgT'); ld(wcT[:].rearrange("p a b -> p (a b)"), wcT_d, 'wcT'); ld(bcT[:], bcT_d, 'bcT')
        ld(sinkb[:], sinkP_d.partition_broadcast(128), 'sinkb')
        ld(dq[:], dq_d, 'dq'); ld(dk[:], dk_d, 'dk')
        ld(g128[:], g128_d.partition_broadcast(128), 'g128'); ld(g64[:], g64_d.partition_broadcast(128), 'g64')
        ld(caus[:], caus_d, 'caus')
        ld(hflag[:], hflag_d, 'hflag')
        P.op('pool', lambda e: e.memset(mskf[:], 0.0), w=['mskf'])
        for pb_ in (0, 64):
            P.op('sp', lambda e, pb_=pb_: e.dma_start(out=mskf[pb_:pb_ + 2, 0:128], in_=sel_d), w=['mskf'], dma=True)
            P.op('sp', lambda e, pb_=pb_: e.dma_start(out=mskf[pb_:pb_ + 2, 128:640], in_=mstd_d), w=['mskf'], dma=True)
            P.op('sp', lambda e, pb_=pb_: e.dma_start(out=mskf[pb_:pb_ + 2, 640:1152], in_=m0_d), w=['mskf'], dma=True)
        P.op('dve', lambda e: e.tensor_copy(out=mskb[:], in_=mskf[:]), r=['mskf'], w=['sel', 'mstd', 'm0'])
        P.op('pool', lambda e: e.memset(Sf[:], 0.0), w=['Sf'])
        P.op('pool', lambda e: e.memset(Sbb[:], 0.0), w=['Sbb'])
        P.op('pool', lambda e: e.memset(carry[:], 0.0), w=['carry'])
        P.op('pool', lambda e: e.memset(ksT[:], 0.0), w=['ksT'])
        P.op('pool', lambda e: e.memset(vstok[:], 0.0), w=['vs0', 'vs1', 'vs2', 'vs3', 'vs4'])

        def conv_w(dst_tile, src, kcn, key):
            P.op('pool', lambda e: e.dma_start(
                out=dst_tile.rearrange("p (kc c) -> p kc c", c=512),
                in_=src.rearrange("(kc p) c -> p kc c", p=128)), w=[key], dma=True)

        conv_done = set()

        def conv_tile(name, t):
            if (name, t) in conv_done:
                return
            conv_done.add((name, t))
            if name == 'mkv':
                conv_w(s_mkv[t], w_mkv[:, t * 512:(t + 1) * 512], 16, ('s', 'mkv', t))
            elif name == 'win':
                if True:
                    conv_w(s_win[t], w_in[:, t * 512:(t + 1) * 512], 16, ('s', 'win', t))
            elif name == 'wbr':
                b_, cg = t // 4, t % 4
                conv_w(s_wbr[t], w_br[b_ * 1024:(b_ + 1) * 1024, cg * 512:(cg + 1) * 512], 8, ('s', 'wbr', t))
            elif name == 'wo':
                conv_w(s_wo[t], w_o[:, t * 512:(t + 1) * 512], 16, ('s', 'wo', t))
            elif name == 'wup':
                conv_w(s_wup[t], w_up[:, t * 512:(t + 1) * 512], 16, ('s', 'wup', t))
            elif name == 'wdn':
                kg, cg = t // 4, t % 4
                conv_w(s_wdn[t], w_dn[kg * 1408:(kg + 1) * 1408, cg * 512:(cg + 1) * 512], 11, ('s', 'wdn', t))

        conv_order = [('win', t) for t in (7, 8, 5, 6, 2)] + [('mkv', t) for t in range(4)] + [('win', t) for t in (0, 1)]
        conv_order += [('win', 13 + i) for i in range(4)] + [('wbr', i) for i in range(4)]
        conv_order += [('win', t) for t in (9, 10, 3, 4)]
        conv_order += [('win', 17 + i) for i in range(4)] + [('wbr', 4 + i) for i in range(4)]
        conv_order += [('win', t) for t in (11, 12)]
        conv_order += [('win', 21 + i) for i in range(4)] + [('wbr', 8 + i) for i in range(4)]
        conv_order += [('wo', t) for t in range(4)]
        for i in range(11):
            conv_order += [('wup', i), ('wup', 11 + i)]
        conv_late = []
        for cg in range(4):
            for kg in range(4):
                conv_late.append(('wdn', kg * 4 + cg))

        def pump(n):
            while n > 0 and conv_order:
                name, t = conv_order.pop(0)
                if (name, t) not in conv_done:
                    conv_tile(name, t)
                    n -= 1

        def skeys(name, t):
            return [('s', name, t)]

        SCR = {'win': s_win, 'mkv': s_mkv, 'wbr': s_wbr, 'wo': s_wo, 'wup': s_wup, 'wdn': s_wdn}
        KCN = {'win': 16, 'mkv': 16, 'wbr': 8, 'wo': 16, 'wup': 16, 'wdn': 11}

        def loadw(name, t):
            conv_tile(name, t)
            j = rot['w'] % 3
            rot['w'] += 1
            n = KCN[name] * 512
            src = SCR[name][t]
            P.op('sp', lambda e: e.dma_start(out=Wp[j][:, 0:n], in_=src), r=skeys(name, t), w=[('W', j)], dma=True)
            return Wp[j][:, 0:n].rearrange("p (kc c) -> p kc c", c=512), ('W', j)

        def norm_part(s, np_, X, xkey):
            P.op('act', lambda e: e.activation(out=Ub[0:np_, :], in_=X, func=AF.Square, accum_out=ss_[0:np_]),
                 r=[xkey], w=['U', 'ss'])
            P.op('act', lambda e: e.activation(out=std_[0:np_], in_=ss_[0:np_], func=AF.Sqrt, scale=1.0 / D, bias=EPS),
                 r=['ss'], w=['std'])
            P.op('dve', lambda e: e.reciprocal(out=rstd_[0:np_], in_=std_[0:np_]), r=['std'], w=['rstd'])
            P.op('dve', lambda e: e.tensor_scalar(out=Ub[0:np_, :], in0=X, scalar1=rstd_[0:np_, 0:1], scalar2=None,
                                                  op0=ALU.mult), r=[xkey, 'rstd'], w=['U'])

        def T_part(s, np_, gT, gkey):
            for hb in range(2):
                t = tb()
                tv = tbank(t)
                for k in range(8):
                    kc = hb * 8 + k
                    P.op('pe', lambda e, k=k, kc=kc, tv=tv: e.transpose(
                        out=tv[:, k, 0:np_], in_=Ub[0:np_, kc * 128:(kc + 1) * 128], identity=ident[0:np_, 0:np_]),
                        r=['U', 'ident'], w=[pk(t)])
                P.op('dve', lambda e, hb=hb, tv=tv: e.tensor_tensor(
                    out=uTb[:, hb * 8:(hb + 1) * 8, s * 128:s * 128 + np_], in0=tv[:, :, 0:np_],
                    in1=gT[:, hb * 8:(hb + 1) * 8].unsqueeze(2).broadcast_to([128, 8, np_]), op=ALU.mult),
                    r=[pk(t), gkey], w=[('uT', s)])

        def norm_T(s, np_, gT, gkey):
            norm_part(s, np_, Xb[0:np_, s, :], ('X', s))
            T_part(s, np_, gT, gkey)

        def load_x(s, np_, src, q='sp'):
            P.op(q, lambda e: e.dma_start(out=Xb[0:np_, s, :], in_=src), w=[('X', s)], dma=True)

        def proj_b(Wv, wkey, c0, nsub, nt, evac, mrows=128):
            b = fb()
            for kc in range(KC):
                P.op('pe', lambda e, kc=kc: e.matmul(PS[0:mrows, b * 512:b * 512 + nt], lhsT=Wv[:, kc, c0:c0 + mrows],
                                                     rhs=uTb[:, kc, 0:nt], start=(kc == 0), stop=(kc == KC - 1)),
                     r=[wkey] + [('uT', s) for s in range(nsub)], w=[pk(b)])
            evac(b)

        def proj_a(Wv, wkey, c0, ncols, s, np_, evac):
            b = fb()
            for kc in range(KC):
                P.op('pe', lambda e, kc=kc: e.matmul(PS[0:np_, b * 512:b * 512 + ncols],
                                                     lhsT=uTb[:, kc, s * 128:s * 128 + np_],
                                                     rhs=Wv[:, kc, c0:c0 + ncols], start=(kc == 0), stop=(kc == KC - 1)),
                     r=[wkey, ('uT', s)], w=[pk(b)])
            evac(b)

        def transpose_tok(src, srckey, np_, nblk, dst_fn, dstkey, eng='act'):
            t = tb()
            tv = tbank(t)
            for k in range(nblk):
                P.op('pe', lambda e, k=k: e.transpose(out=tv[:, k, 0:np_], in_=src[0:np_, k * 128:(k + 1) * 128],
                                                      identity=ident[0:np_, 0:np_]),
                     r=[srckey, 'ident'], w=[pk(t)])
            return t, tv

        def rotary(b, np_, cs, cskey, dtab, dkey, out_bf, outkey, peng='pool', scr=None):
            pv = PS[0:np_, b * 512:(b + 1) * 512].rearrange("p (h t d) -> p h t d", t=2, d=64)
            x1 = pv[:, :, 0, :]
            x2 = pv[:, :, 1, :]
            cosb = cs[0:np_, 0:64].unsqueeze(1).broadcast_to([np_, 4, 64])
            sinb = cs[0:np_, 64:128].unsqueeze(1).broadcast_to([np_, 4, 64])
            if scr is None:
                rts, qfv, kp_ = rt, qf, ''
            else:
                rts, qfv, kp_ = scr
            qf_ = qfv
            t1, t2, t3, t4 = [r_[0:np_] for r_ in rts]
            P.op('dve', lambda e: e.tensor_tensor(out=t1, in0=x1, in1=cosb, op=ALU.mult), r=[pk(b), cskey], w=[kp_ + 'rt0'])
            P.op('dve', lambda e: e.tensor_tensor(out=t2, in0=x2, in1=sinb, op=ALU.mult), r=[pk(b), cskey], w=[kp_ + 'rt1'])
            P.op('dve', lambda e: e.tensor_tensor(out=t3, in0=x1, in1=sinb, op=ALU.mult), r=[pk(b), cskey], w=[kp_ + 'rt2'])
            P.op('dve', lambda e: e.tensor_tensor(out=t4, in0=x2, in1=cosb, op=ALU.mult), r=[pk(b), cskey], w=[kp_ + 'rt3'])
            P.op(peng, lambda e: e.tensor_tensor(out=qf_[0:np_, :, 0, :], in0=t1, in1=t2, op=ALU.subtract),
                 r=[kp_ + 'rt0', kp_ + 'rt1'], w=[kp_ + 'qf'])
            P.op(peng, lambda e: e.tensor_tensor(out=qf_[0:np_, :, 1, :], in0=t3, in1=t4, op=ALU.add),
                 r=[kp_ + 'rt2', kp_ + 'rt3'], w=[kp_ + 'qf'])
            P.op(peng, lambda e: e.tensor_tensor(
                out=out_bf, in0=qf_[0:np_].rearrange("p h t d -> p h (t d)"),
                in1=dtab.unsqueeze(2).broadcast_to([np_, 4, 128]), op=ALU.mult),
                r=[kp_ + 'qf', dkey], w=[outkey])

        def ret_dS(s, np_):
            for h in range(8):
                P.op('pe', lambda e, h=h: e.matmul(PS[:, 1024 + h * 128:1024 + (h + 1) * 128],
                                                   lhsT=kt[0:np_, s, h * 128:(h + 1) * 128],
                                                   rhs=vr[0:np_, s, h * 128:(h + 1) * 128], start=True, stop=True),
                     r=[('kt', s, h // 4), 'vr'], w=[pk(2 + h // 4)])

        def ret_state_fin(s, np_, gC, gkey, peng='pool'):
            dS = PS[:, 1024:2048].rearrange("p (h d) -> p h d", d=128)
            gb = gC[:, :].unsqueeze(2).broadcast_to([128, 8, 128])
            P.op('dve', lambda e: e.tensor_tensor(out=Sf[:], in0=dS, in1=Sf[:], op=ALU.add),
                 r=[pk(2), pk(3), 'Sf'], w=['Sf'])
            P.op('dve', lambda e: e.tensor_tensor(out=Sbb[:], in0=Sf[:], in1=gb, op=ALU.mult),
                 r=['Sf', gkey], w=['Sbb'])
            P.op(peng, lambda e: e.tensor_tensor(out=Sf[:], in0=Sf[:], in1=gb, op=ALU.mult),
                 r=['Sf', gkey], w=['Sf'])

        def ret_state_update(s, np_, gC, gkey):
            ret_dS(s, np_)
            ret_state_fin(s, np_, gC, gkey, peng='dve')

        def load_cs(src):
            i = rot['cs'] % 2
            rot['cs'] += 1
            P.op('sp', lambda e: e.dma_start(out=cst[i][0:src.shape[0], :], in_=src), w=[('cs', i)], dma=True)
            return cst[i], ('cs', i)

        def swa_kv_proj(W2, k2, subs, np_, nsub, nt, out_k=None, out_v=None, out_sub=None):
            for blk in range(2):
                def ev(b, blk=blk):
                    P.op('act', lambda e: e.activation(out=ksT[:, blk, 128:128 + nt], in_=PS[:, b * 512:b * 512 + nt],
                                                       func=AF.Copy), r=[pk(b)], w=['ksT'])
                proj_b(W2, k2, blk * 128, nsub, nt, ev)
            for s in subs:
                def ev(b, s=s):
                    P.op('act', lambda e: e.activation(out=vstok[0:np_, 1 + s, :], in_=PS[0:np_, b * 512:b * 512 + 256],
                                                       func=AF.Copy), r=[pk(b)], w=['vs%d' % (1 + s)])
                    if out_v is not None and s == out_sub:
                        i = s % 2
                        P.op('dve', lambda e: e.tensor_copy(out=kvst[i][0:np_, :], in_=PS[0:np_, b * 512:b * 512 + 256]),
                             r=[pk(b)], w=['kvst%d' % i])
                        P.op('pool', lambda e: e.dma_start(out=out_v, in_=kvst[i][0:np_, :]), r=['kvst%d' % i], dma=True)
                proj_a(W2, k2, 256, 256, s, np_, ev)
            if out_k is not None:
                s = out_sub
                def ev(b):
                    P.op('dve', lambda e: e.tensor_copy(out=kvst[1][0:np_, :], in_=PS[0:np_, b * 512:b * 512 + 256]),
                         r=[pk(b)], w=['kvst1'])
                    P.op('pool', lambda e: e.dma_start(out=out_k, in_=kvst[1][0:np_, :]), r=['kvst1'], dma=True)
                proj_a(W2, k2, 0, 256, s, np_, ev)

        def evac_PT(t, tv, n, dst0, np_):
            P.op('act', lambda e: e.activation(out=PTs[:, dst0:dst0 + n, 0:np_], in_=tv[:, 0:n, 0:np_], func=AF.Copy),
                 r=[pk(t)], w=['PTs'])

        def otok_to_obT(s, np_):
            t, tv = transpose_tok(otok, 'otok', np_, 8, None, None)
            P.op('act', lambda e: e.activation(out=obT[:, :, s * 128:s * 128 + np_], in_=tv[:, :, 0:np_], func=AF.Copy),
                 r=[pk(t)], w=['obT'])

        def otok_T(oi, s, np_):
            t, tv = transpose_tok(otok2[oi], 'otok%d' % oi, np_, 8, None, None)
            P.op('act', lambda e: e.activation(out=obT[:, :, s * 128:s * 128 + np_], in_=tv[:, :, 0:np_], func=AF.Copy),
                 r=[pk(t)], w=['obT'])

        def attn_all(kind, subs, np_, nk, mask_for=None):
            nr = 4 if kind == 'swa' else 1
            rounds = [(s, r) for s in subs for r in range(nr)]
            nkh = (nk + 127) // 128
            dh = 64 if kind == 'swa' else 256
            nh = 16 if kind == 'swa' else 4
            deferred = []

            def S(idx):
                s, r = rounds[idx]
                par = idx % 2
                if kind == 'swa':
                    maskb, maskkey = mask_for(s)
                    k0 = s * 128
                    t = r // 2
                    pb = (r % 2) * 64
                    for half in range(2):
                        bk = par * 2 + half
                        P.op('pe', lambda e, bk=bk, pb=pb, maskb=maskb: e.matmul(
                            PS[0:np_, bk * 512:(bk + 1) * 512], lhsT=sel[pb:pb + 2, 0:np_], rhs=maskb[pb:pb + 2, :],
                            start=True, stop=False), r=['sel', maskkey], w=[pk(bk)])
                        for hf in range(2):
                            a = half * 2 + hf
                            P.op('pe', lambda e, bk=bk, pb=pb, hf=hf, a=a, t=t, s=s, k0=k0: e.matmul(
                                PS[0:np_, bk * 512 + hf * 256:bk * 512 + hf * 256 + nk],
                                lhsT=QT[pb:pb + 64, t * 4 + a, s * 128:s * 128 + np_],
                                rhs=ksT[pb:pb + 64, t, k0:k0 + nk], start=False, stop=(hf == 1)),
                                r=['QT', 'ksT'], w=[pk(bk)])
                else:
                    for h in range(4):
                        bk = par * 2 + h // 2
                        for dc in range(2):
                            P.op('pe', lambda e, h=h, dc=dc, bk=bk, s=s: e.matmul(
                                PS[0:np_, bk * 512 + (h % 2) * 256:bk * 512 + (h % 2) * 256 + 256],
                                lhsT=QT[:, h * 2 + dc, s * 128:s * 128 + np_],
                                rhs=mkT[:, h * 2 + dc, :], start=(dc == 0), stop=(dc == 1)),
                                r=['QT', 'mkT'], w=[pk(bk)])

            def SM(idx):
                s, r = rounds[idx]
                par = idx % 2
                base = par * 1024
                S4 = PS[0:np_, base:base + 1024].rearrange("p (h k) -> p h k", k=256)[:, :, 0:nk]
                mx = stp[0:np_, par, 0:4]; nmx = stp[0:np_, par, 4:8]; rs = stp[0:np_, par, 8:12]
                es = stp[0:np_, par, 12:16]; den = stp[0:np_, par, 16:20]
                kk = lambda n: (n, par)
                bks = [pk(par * 2), pk(par * 2 + 1)]
                ri = rinv2[0:np_, s % 2, r * 4:(r + 1) * 4]
                P.op('dve', lambda e: e.tensor_reduce(out=mx, in_=S4, axis=AX.X, op=ALU.max), r=bks, w=[kk('mx')])
                if kind == 'swa':
                    sk = sinkb[0:np_, r * 4:(r + 1) * 4]
                    P.op('dve', lambda e: e.tensor_tensor(out=mx, in0=mx, in1=sk, op=ALU.max),
                         r=[kk('mx'), 'sinkb'], w=[kk('mx')])
                P.op('dve', lambda e: e.tensor_scalar(out=nmx, in0=mx, scalar1=-1.0, scalar2=None, op0=ALU.mult),
                     r=[kk('mx')], w=[kk('nmx')])
                for i in range(4):
                    P.op('act', lambda e, i=i: e.activation(
                        out=Pb[0:np_, par * 4 + i, 0:nk], in_=PS[0:np_, base + i * 256:base + i * 256 + nk],
                        func=AF.Exp, bias=stp[0:np_, par, 4 + i:5 + i], accum_out=stp[0:np_, par, 8 + i:9 + i]),
                        r=[pk(par * 2 + i // 2), kk('nmx')], w=[('Pb', par), kk('rs')])
                if kind == 'swa':
                    P.op('dve', lambda e: e.tensor_tensor(out=es, in0=sk, in1=nmx, op=ALU.add),
                         r=['sinkb', kk('nmx')], w=[kk('es')])
                    P.op('act', lambda e: e.activation(out=es, in_=es, func=AF.Exp), r=[kk('es')], w=[kk('es')])

            def SM2(idx):
                s, r = rounds[idx]
                par = idx % 2
                rs = stp[0:np_, par, 8:12]
                es = stp[0:np_, par, 12:16]; den = stp[0:np_, par, 16:20]
                kk = lambda n: (n, par)
                ri = rinv2[0:np_, s % 2, r * 4:(r + 1) * 4]
                if kind == 'swa':
                    P.op('dve', lambda e: e.tensor_tensor(out=den, in0=rs, in1=es, op=ALU.add),
                         r=[kk('rs'), kk('es')], w=[kk('den')])
                    P.op('dve', lambda e: e.reciprocal(out=ri, in_=den), r=[kk('den')], w=[('rinv', s % 2)])
                else:
                    P.op('dve', lambda e: e.reciprocal(out=ri, in_=rs), r=[kk('rs')], w=[('rinv', s % 2)])

            def TP(idx):
                s, r = rounds[idx]
                par = idx % 2
                tt = tb()
                tv = tbank(tt)
                for i in range(4):
                    for kh in range(nkh):
                        kw = min(128, nk - kh * 128)
                        P.op('pe', lambda e, i=i, kh=kh, kw=kw: e.transpose(
                            out=tv[0:kw, i * 2 + kh, 0:np_], in_=Pb[0:np_, par * 4 + i, kh * 128:kh * 128 + kw],
                            identity=ident[0:np_, 0:np_]), r=[('Pb', par), 'ident'], w=[pk(tt)])
                P.op('act', lambda e: e.activation(out=PTs[:, par * 8:par * 8 + 8, 0:np_], in_=tv[:, 0:8, 0:np_],
                                                   func=AF.Copy), r=[pk(tt)], w=[('PTs', par)])

            def PV(idx):
                s, r = rounds[idx]
                par = idx % 2
                for i in range(4):
                    h = r * 4 + i
                    for kh in range(nkh):
                        kw = min(128, nk - kh * 128)
                        if kind == 'swa':
                            rhs = vstok[0:kw, s + kh, r * 64:(r + 1) * 64]
                            rkey = 'vs%d' % (s + kh)
                        else:
                            rhs = mvtok[:, kh, i * 256:(i + 1) * 256]
                            rkey = 'mvtok'
                        P.op('pe', lambda e, i=i, h=h, kh=kh, kw=kw, rhs=rhs: e.matmul(
                            PS[0:np_, 2048 + h * dh:2048 + (h + 1) * dh], lhsT=PTs[0:kw, par * 8 + i * 2 + kh, 0:np_],
                            rhs=rhs, start=(kh == 0), stop=(kh == nkh - 1)),
                            r=[('PTs', par), rkey], w=[pk(4 + (h * dh) // 512)])
                if r == nr - 1:
                    FIN(s)

            def FIN(s):
                oi = s % 2
                O = PS[0:np_, 2048:3072].rearrange("p (h d) -> p h d", d=dh)
                P.op('dve', lambda e: e.tensor_tensor(
                    out=otok2[oi][0:np_, :].rearrange("p (h d) -> p h d", d=dh), in0=O,
                    in1=rinv2[0:np_, s % 2, 0:nh].unsqueeze(2).broadcast_to([np_, nh, dh]), op=ALU.mult),
                    r=[pk(4), pk(5), ('rinv', s % 2)], w=['otok%d' % oi])
                deferred.append(lambda: otok_T(oi, s, np_))

            S(0)
            SM(0)
            for idx in range(len(rounds)):
                if idx + 1 < len(rounds):
                    S(idx + 1)
                    SM(idx + 1)
                SM2(idx)
                TP(idx)
                pend = list(deferred)
                del deferred[:]
                if idx >= 1:
                    PV(idx - 1)
                for f_ in pend:
                    f_()
            PV(len(rounds) - 1)
            while deferred:
                deferred.pop(0)()

        ret_pending = []

        def ret_A(s, np_):
            c0 = s * 128
            for h in range(8):
                P.op('pe', lambda e, h=h: e.matmul(PS[0:np_, h * 128:h * 128 + np_], lhsT=krT[:, h, c0:c0 + np_],
                                                   rhs=QT[:, h, c0:c0 + np_], start=True, stop=True),
                     r=['krT', 'QT'], w=[pk(h // 4)])
            ai = s % 2
            AT = Pb[:, :, :].rearrange("p a b -> p (a b)")[:, ai * 1024:(ai + 1) * 1024].rearrange("p (h n) -> p h n", n=128)
            A_ps = PS[0:np_, 0:1024].rearrange("p (h n) -> p h n", n=128)[:, :, 0:np_]
            P.op('dve', lambda e: e.tensor_tensor(out=AT[0:np_, :, 0:np_], in0=A_ps,
                                                  in1=caus[0:np_, 0:np_].unsqueeze(1).broadcast_to([np_, 8, np_]),
                                                  op=ALU.mult), r=[pk(0), pk(1), 'caus'], w=[('AT', ai)])

        def ret_attend(s, np_, gC, gkey):
            c0 = s * 128
            ai = s % 2
            AT = Pb[:, :, :].rearrange("p a b -> p (a b)")[:, ai * 1024:(ai + 1) * 1024].rearrange("p (h n) -> p h n", n=128)
            for h in range(8):
                P.op('pe', lambda e, h=h: e.matmul(PS[0:np_, 2048 + h * 128:2048 + (h + 1) * 128],
                                                   lhsT=AT[0:np_, h, 0:np_], rhs=vr[0:np_, s, h * 128:(h + 1) * 128],
                                                   start=(h % 4 == 0), stop=False, skip_group_check=True),
                     r=[('AT', ai), 'vr'], w=[pk(4 + h // 4)])
            ret_dS(s, np_)
            for h in range(8):
                P.op('pe', lambda e, h=h: e.matmul(PS[0:np_, 2048 + h * 128:2048 + (h + 1) * 128],
                                                   lhsT=QT[:, h, c0:c0 + np_], rhs=Sbb[:, h, :],
                                                   start=False, stop=True, skip_group_check=True),
                     r=['QT', 'Sbb'], w=[pk(4 + h // 4)])
            ret_state_fin(s, np_, gC, gkey)
            O = PS[0:np_, 2048:3072].rearrange("p (h d) -> p h d", d=128)
            for h in range(8):
                P.op('act', lambda e, h=h: e.activation(out=qtok[0:np_, 0, :], in_=O[:, h, :], func=AF.Square,
                                                        accum_out=ssq8[0:np_, h:h + 1]),
                     r=[pk(4 + h // 4)], w=['qtok', 'ssq8'])
            P.op('act', lambda e: e.activation(out=std8[0:np_], in_=ssq8[0:np_], func=AF.Sqrt, scale=1.0 / 128, bias=EPS),
                 r=['ssq8'], w=['std8'])
            P.op('dve', lambda e: e.reciprocal(out=rstd8[0:np_], in_=std8[0:np_]), r=['std8'], w=['rstd8'])
            P.op('dve', lambda e: e.tensor_tensor(out=O, in0=O, in1=rstd8[0:np_].unsqueeze(2).broadcast_to([np_, 8, 128]),
                                                  op=ALU.mult), r=[pk(4), pk(5), 'rstd8'], w=[pk(4), pk(5)])
            oi = s % 2
            P.op('dve', lambda e: e.tensor_tensor(out=otok2[oi][0:np_, :], in0=PS[0:np_, 2048:3072], in1=gr[0:np_, s, :],
                                                  op=ALU.mult), r=[pk(4), pk(5), 'gr'], w=['otok%d' % oi])
            ret_pending.append(lambda: otok_T(oi, s, np_))

        def merge(b, nsub, nt):
            for cg in range(4):
                Wg, kg_ = loadw('win', 13 + b * 4 + cg)
                Wb_, kb_ = loadw('wbr', b * 4 + cg)
                for dd in range(4):
                    dc = cg * 4 + dd
                    gi = dd % 2
                    bg = fb()
                    for kc in range(KC):
                        P.op('pe', lambda e, kc=kc, dd=dd, bg=bg, Wg=Wg: e.matmul(
                            PS[:, bg * 512:bg * 512 + nt], lhsT=Wg[:, kc, dd * 128:(dd + 1) * 128],
                            rhs=uTb[:, kc, 0:nt], start=(kc == 0), stop=(kc == KC - 1)),
                            r=[kg_] + [('uT', s) for s in range(nsub)], w=[pk(bg)])
                    P.op('act', lambda e, gi=gi, dc=dc, bg=bg: e.activation(
                        out=gsb[gi][:, 0:nt], in_=PS[:, bg * 512:bg * 512 + nt], func=AF.Sigmoid,
                        bias=bgT[:, b * 16 + dc:b * 16 + dc + 1]), r=[pk(bg), 'bgT'], w=['gsb%d' % gi])
                    by = fb()
                    for kc in range(8):
                        P.op('pe', lambda e, kc=kc, dd=dd, by=by, Wb_=Wb_: e.matmul(
                            PS[:, by * 512:by * 512 + nt], lhsT=Wb_[:, kc, dd * 128:(dd + 1) * 128],
                            rhs=obT[:, kc, 0:nt], start=(kc == 0), stop=(kc == 7)),
                            r=[kb_, 'obT'], w=[pk(by)])
                    if b == 0:
                        P.op('dve', lambda e, gi=gi, dc=dc, by=by: e.tensor_tensor(
                            out=mT[:, dc, 0:nt], in0=PS[:, by * 512:by * 512 + nt], in1=gsb[gi][:, 0:nt], op=ALU.mult),
                            r=[pk(by), 'gsb%d' % gi], w=[('mT', dc)])
                    else:
                        P.op('dve', lambda e, gi=gi, by=by: e.tensor_tensor(
                            out=ytmp[gi][:, 0:nt], in0=PS[:, by * 512:by * 512 + nt], in1=gsb[gi][:, 0:nt], op=ALU.mult),
                            r=[pk(by), 'gsb%d' % gi], w=['ytmp%d' % gi])
                        P.op('pool', lambda e, gi=gi, dc=dc: e.tensor_tensor(
                            out=mT[:, dc, 0:nt], in0=mT[:, dc, 0:nt], in1=ytmp[gi][:, 0:nt], op=ALU.add),
                            r=['ytmp%d' % gi, ('mT', dc)], w=[('mT', dc)])

        import os as _os2
        CUT = int(_os2.environ.get('KCUT', '99'))
        CUT2 = int(_os2.environ.get('KCUT2', '99'))

        def full_tile(xsrc, cssrc, nsub, np_, first_mask=False, sample=False, do_down=True, y_dst=None,
                      k_out=None, v_out=None, prenormed=False, next_pre=None, xq='sp'):
            nt = nsub * np_ if not sample else np_
            subs = list(range(nsub))
            nk = 192 if sample else 256
            gC, gkey = (g64, 'g64') if sample else (g128, 'g128')
            fence(RETK + FFNK, XK + MIXK)
            if not prenormed:
                for s in subs:
                    load_x(s, np_, xsrc[s * 128:s * 128 + np_, :])
                    norm_T(s, np_, gmixT, 'gmixT')
            fence(XK, RETK)
            if CUT <= 1:
                return
            for t in range(2):
                Wq, kq_ = loadw('win', t)
                for a in range(4):
                    def ev(b, t=t, a=a):
                        P.op('act', lambda e: e.activation(out=QT[:, t * 4 + a, 0:nt], in_=PS[:, b * 512:b * 512 + nt],
                                                           func=AF.Copy, scale=0.125), r=[pk(b)], w=['QT'])
                    proj_b(Wq, kq_, a * 128, nsub, nt, ev)
            W2, k2 = loadw('win', 2)
            swa_kv_proj(W2, k2, subs, np_, nsub, nt, out_k=k_out, out_v=v_out, out_sub=nsub - 1)
            if CUT <= 2:
                return
            attn_all('swa', subs, np_, nk,
                     mask_for=lambda s_: (m0, 'm0') if (first_mask and s_ == 0) else (mstd, 'mstd'))
            if CUT <= 3:
                return
            merge(0, nsub, nt)
            if do_down and not sample:
                while conv_late:
                    conv_tile(*conv_late.pop(0))
            if CUT <= 4:
                return
            if not sample:
                P.op('pool', lambda e: e.tensor_copy(out=ksT[:, :, 0:128], in_=ksT[:, :, nt:nt + 128]),
                     r=['ksT'], w=['ksT'])
                P.op('pool', lambda e: e.tensor_copy(out=vstok[:, 0, :], in_=vstok[:, nsub, :]),
                     r=['vs%d' % nsub], w=['vs0'])
            for wt in (7, 8):
                Wv, kv_ = loadw('win', wt)
                for s in subs:
                    def ev(b, s=s, wt=wt):
                        P.op('act', lambda e: e.activation(out=vr[0:np_, s, (wt - 7) * 512:(wt - 6) * 512],
                                                           in_=PS[0:np_, b * 512:(b + 1) * 512], func=AF.Copy),
                             r=[pk(b)], w=['vr'])
                    proj_a(Wv, kv_, 0, 512, s, np_, ev)
            for wt in (9, 10):
                Wv, kv_ = loadw('win', wt)
                for s in subs:
                    def ev(b, s=s, wt=wt):
                        P.op('act', lambda e: e.activation(out=gr[0:np_, s, (wt - 9) * 512:(wt - 8) * 512],
                                                           in_=PS[0:np_, b * 512:(b + 1) * 512], func=AF.Silu),
                             r=[pk(b)], w=['gr'])
                    proj_a(Wv, kv_, 0, 512, s, np_, ev)
            cs_tabs = {}
            pending = []

            def flush(keep):
                while len(pending) > keep:
                    pending.pop(0)()
            for wt in (5, 6, 3, 4):
                Wv, kv_ = loadw('win', wt)
                isk = wt in (5, 6)
                hq = (wt - 5) if isk else (wt - 3)
                for s in subs:
                    if s not in cs_tabs or True:
                        cs_tabs[s] = load_cs(cssrc[s * 128:s * 128 + np_, :])
                    cs, cskey = cs_tabs[s]
                    def ev(b, s=s, isk=isk, hq=hq, cs=cs, cskey=cskey):
                        if isk:
                            dst = kt[0:np_, s, hq * 512:(hq + 1) * 512].rearrange("p (h d) -> p h d", d=128)
                            rotary(b, np_, cs, cskey, dk[0:np_, hq * 4:(hq + 1) * 4], 'dk', dst, ('kt', s, hq))
                            def later(s=s, hq=hq):
                                src = kt[:, s, hq * 512:(hq + 1) * 512]
                                t, tv = transpose_tok(src, ('kt', s, hq), np_, 4, None, None)
                                P.op('act', lambda e: e.activation(out=krT[:, hq * 4:(hq + 1) * 4, s * 128:s * 128 + np_],
                                                                   in_=tv[:, 0:4, 0:np_], func=AF.Copy),
                                     r=[pk(t)], w=['krT'])
                            pending.append(later)
                        else:
                            qi = rot['q'] % 4
                            rot['q'] += 1
                            rotary(b, np_, cs, cskey, dq[0:np_, hq * 4:(hq + 1) * 4], 'dq', qtok2[qi][0:np_], 'qtok%d' % qi)

                            def later(s=s, hq=hq, qi=qi):
                                src = qtok2[qi][:, :, :].rearrange("p a b -> p (a b)")
                                t, tv = transpose_tok(src, 'qtok%d' % qi, np_, 4, None, None)
                                P.op('act', lambda e: e.activation(out=QT[:, hq * 4:(hq + 1) * 4, s * 128:s * 128 + np_],
                                                                   in_=tv[:, 0:4, 0:np_], func=AF.Copy),
                                     r=[pk(t)], w=['QT'])
                            pending.append(later)
                    proj_a(Wv, kv_, 0, 512, s, np_, ev)
                    flush(3)
            flush(0)
            if CUT <= 5:
                return
            ret_A(0, np_)
            for s in subs:
                if s + 1 < nsub:
                    ret_A(s + 1, np_)
                ret_attend(s, np_, gC, gkey)
                while len(ret_pending) > 1:
                    ret_pending.pop(0)()
            while ret_pending:
                ret_pending.pop(0)()
            fence(RETK, XK)
            for s in subs:
                load_x(s, np_, xsrc[s * 128:s * 128 + np_, :], q=xq)
            if CUT <= 6:
                return
            merge(1, nsub, nt)
            if CUT <= 7:
                return
            for wt in (11, 12):
                Wq, kq_ = loadw('win', wt)
                for a in range(4):
                    def ev(b, wt=wt, a=a):
                        P.op('act', lambda e: e.activation(out=QT[:, (wt - 11) * 4 + a, 0:nt],
                                                           in_=PS[:, b * 512:b * 512 + nt],
                                                           func=AF.Copy, scale=1.0 / 16), r=[pk(b)], w=['QT'])
                    proj_b(Wq, kq_, a * 128, nsub, nt, ev)
            attn_all('mem', subs, np_, 256)
            merge(2, nsub, nt)
            if CUT <= 8:
                return
            for cg in range(4):
                Wo, ko_ = loadw('wo', cg)
                for s in subs:
                    b = fb()
                    for kc in range(KC):
                        P.op('pe', lambda e, kc=kc, s=s, b=b, Wo=Wo: e.matmul(
                            PS[0:np_, b * 512:(b + 1) * 512], lhsT=mT[:, kc, s * 128:s * 128 + np_],
                            rhs=Wo[:, kc, :], start=(kc == 0), stop=(kc == KC - 1)),
                            r=[ko_, ('mT', kc)], w=[pk(b)])
                    if cg == 3 and s >= 1:
                        T_part(s - 1, np_, gffnT, 'gffnT')
                    P.op('dve', lambda e, s=s, b=b, cg=cg: e.tensor_tensor(
                        out=Xb[0:np_, s, cg * 512:(cg + 1) * 512], in0=PS[0:np_, b * 512:(b + 1) * 512],
                        in1=Xb[0:np_, s, cg * 512:(cg + 1) * 512], op=ALU.add), r=[pk(b), ('X', s)], w=[('X', s)])
                    if cg == 3:
                        norm_part(s, np_, Xb[0:np_, s, :], ('X', s))
            if CUT <= 9:
                return
            T_part(nsub - 1, np_, gffnT, 'gffnT')
            fence(MIXK, FFNK)
            for i in range(11):
                Wg, kg_ = loadw('wup', i)
                Wv, kv_ = loadw('wup', 11 + i)
                if do_down and i == 2:
                    P.op('sp', lambda e: e.dma_start(out=gfin[:, :], in_=gfin_d.partition_broadcast(128)),
                         w=['gfin'], dma=True)
                for dd in range(4):
                    j = i * 4 + dd
                    cts = []
                    for which, (Wx, kx_) in enumerate(((Wg, kg_), (Wv, kv_))):
                        jb = j + 44 * which
                        ab = abuf[which][j % 2]
                        akey = 'abuf%d%d' % (which, j % 2)
                        ct = ctmp[which]
                        ckey = 'ctmp%d' % which
                        b = fb()
                        for kc in range(KC):
                            P.op('pe', lambda e, kc=kc, dd=dd, Wx=Wx, b=b: e.matmul(
                                PS[:, b * 512:b * 512 + nt], lhsT=Wx[:, kc, dd * 128:(dd + 1) * 128],
                                rhs=uTb[:, kc, 0:nt], start=(kc == 0), stop=(kc == KC - 1)),
                                r=[kx_] + [('uT', s) for s in subs], w=[pk(b)])
                        P.op('pool', lambda e, ab=ab, jb=jb: e.tensor_copy(out=ab[:, 0:2], in_=carry[:, jb, :]),
                             r=['carry'], w=[akey])
                        P.op('act', lambda e, ab=ab, b=b: e.activation(out=ab[:, 2:2 + nt], in_=PS[:, b * 512:b * 512 + nt],
                                                                       func=AF.Copy), r=[pk(b)], w=[akey])
                        P.op('pool', lambda e, ab=ab, jb=jb: e.tensor_copy(out=carry[:, jb, :], in_=ab[:, nt:nt + 2]),
                             r=[akey], w=['carry'])
                        P.op('dve', lambda e, ab=ab, jb=jb, ct=ct: e.tensor_scalar(
                            out=ct[:, 0:nt], in0=ab[:, 2:2 + nt], scalar1=wcT[:, jb, 2:3], scalar2=bcT[:, jb:jb + 1],
                            op0=ALU.mult, op1=ALU.add), r=[akey, 'wcT', 'bcT'], w=[ckey])
                        P.op('dve', lambda e, ab=ab, jb=jb, ct=ct: e.scalar_tensor_tensor(
                            out=ct[:, 0:nt], in0=ab[:, 1:1 + nt], scalar=wcT[:, jb, 1:2], in1=ct[:, 0:nt],
                            op0=ALU.mult, op1=ALU.add), r=[akey, 'wcT', ckey], w=[ckey])
                        P.op('dve', lambda e, ab=ab, jb=jb, ct=ct: e.scalar_tensor_tensor(
                            out=ct[:, 0:nt], in0=ab[:, 0:nt], scalar=wcT[:, jb, 0:1], in1=ct[:, 0:nt],
                            op0=ALU.mult, op1=ALU.add), r=[akey, 'wcT', ckey], w=[ckey])
                    if do_down:
                        P.op('act', lambda e: e.activation(out=sgt[:, 0:nt], in_=ctmp[0][:, 0:nt], func=AF.Silu),
                             r=['ctmp0'], w=['sgt'])
                        P.op('dve', lambda e, j=j: e.tensor_tensor(out=hT[:, j, 0:nt], in0=sgt[:, 0:nt],
                                                                   in1=ctmp[1][:, 0:nt], op=ALU.mult),
                             r=['sgt', 'ctmp1'], w=[('hT', j)])
            if not do_down or CUT <= 10:
                return
            pre_sched = {}
            if next_pre is not None:
                nx_src, nx_nsub, nx_np = next_pre
                for s2 in range(nx_nsub):
                    pre_sched.setdefault(4 + s2, []).append(('norm', s2))
                    pre_sched.setdefault(5 + s2, []).insert(0, ('T', s2))
            blk = 0
            for cg in range(4):
                for kg in range(4):
                    Wd, kd_ = loadw('wdn', kg * 4 + cg)
                    for s in subs:
                        for k in range(11):
                            kc = kg * 11 + k
                            P.op('pe', lambda e, k=k, kc=kc, s=s, Wd=Wd: e.matmul(
                                PS[0:np_, s * 512:(s + 1) * 512], lhsT=hT[:, kc, s * 128:s * 128 + np_],
                                rhs=Wd[:, k, :], start=(kc == 0), stop=(kc == 43)),
                                r=[kd_, ('hT', kc)], w=[pk(s)])
                    for what, s2 in pre_sched.get(blk, []):
                        if what == 'norm':
                            P.op('sp', lambda e, s2=s2: e.dma_start(out=Xn[0:nx_np, :],
                                                                   in_=nx_src[s2 * 128:s2 * 128 + nx_np, :]),
                                 w=['Xn'], dma=True)
                            norm_part(s2, nx_np, Xn[0:nx_np, :], 'Xn')
                        else:
                            T_part(s2, nx_np, gmixT, 'gmixT')
                    blk += 1
                for s in subs:
                    P.op('dve', lambda e, s=s, cg=cg: e.tensor_tensor(
                        out=Xb[0:np_, s, cg * 512:(cg + 1) * 512], in0=PS[0:np_, s * 512:(s + 1) * 512],
                        in1=Xb[0:np_, s, cg * 512:(cg + 1) * 512], op=ALU.add), r=[pk(s), ('X', s)], w=[('X', s)])
            for s in subs:
                X = Xb[0:np_, s, :]
                P.op('act', lambda e, X=X: e.activation(out=Ub[0:np_, :], in_=X, func=AF.Square, accum_out=ss_[0:np_]),
                     r=[('X', s)], w=['U', 'ss'])
                P.op('act', lambda e: e.activation(out=std_[0:np_], in_=ss_[0:np_], func=AF.Sqrt, scale=1.0 / D, bias=EPS),
                     r=['ss'], w=['std'])
                P.op('dve', lambda e: e.reciprocal(out=rstd_[0:np_], in_=std_[0:np_]), r=['std'], w=['rstd'])
                P.op('dve', lambda e, X=X: e.scalar_tensor_tensor(out=X, in0=X, scalar=rstd_[0:np_, 0:1], in1=gfin[0:np_, :],
                                                                  op0=ALU.mult, op1=ALU.mult),
                     r=[('X', s), 'rstd', 'gfin'], w=[('X', s)])
                P.op('pool', lambda e, X=X, s=s: e.dma_start(out=y_dst[s * 128:s * 128 + np_, :], in_=X),
                     r=[('X', s)], dma=True)

        def mem_pass():
            fence(RETK + FFNK, XK + MIXK, eng='dve')
            pump(8)
            for s in range(2):
                load_x(s, 128, mem[s * 128:(s + 1) * 128, :])
                norm_T(s, 128, gmemT, 'gmemT')
                pump(2)
            for t in range(4):
                Wv, kv_ = loadw('mkv', t)
                if t < 2:
                    for a in range(4):
                        def ev(b, t=t, a=a):
                            P.op('act', lambda e: e.activation(out=mkT[:, t * 4 + a, :], in_=PS[:, b * 512:b * 512 + 256],
                                                               func=AF.Copy), r=[pk(b)], w=['mkT'])
                        proj_b(Wv, kv_, a * 128, 2, 256, ev)
                for s in range(2):
                    def ev(b, t=t, s=s):
                        i = s % 2
                        if t >= 2:
                            P.op('act', lambda e: e.activation(out=mvtok[:, s, (t - 2) * 512:(t - 1) * 512],
                                                               in_=PS[:, b * 512:(b + 1) * 512], func=AF.Copy),
                                 r=[pk(b)], w=['mvtok'])
                        P.op('dve', lambda e: e.tensor_copy(out=ctmp[i][:, :], in_=PS[:, b * 512:(b + 1) * 512]),
                             r=[pk(b)], w=['ctmp%d' % i])
                        dst = (mk_o if t < 2 else mv_o)[s * 128:(s + 1) * 128, (t % 2) * 512:(t % 2 + 1) * 512]
                        P.op('pool', lambda e: e.dma_start(out=dst, in_=ctmp[i][:, :]), r=['ctmp%d' % i], dma=True)
                    proj_a(Wv, kv_, 0, 512, s, 128, ev)

        def prefix_pass():
            NS = 31
            WRK = [('Wres', i) for i in range(4)]
            XSK = ['xrt0', 'xrt1', 'xrt2', 'xrt3', 'xqf']
            fence(MIXK + FFNK + ['Xn'], WRK + XSK, eng='dve')
            Wres = {}
            for i, wt in enumerate((7, 8, 5, 6)):
                conv_tile('win', wt)
                dstw = OV[:, i * 8192:(i + 1) * 8192]
                P.op('sp', lambda e, dstw=dstw, wt=wt: e.dma_start(out=dstw, in_=s_win[wt]), r=skeys('win', wt),
                     w=[('Wres', i)], dma=True)
                Wres[wt] = (dstw.rearrange("p (kc c) -> p kc c", c=512), ('Wres', i))
            xscr = ([Xn[:, i * 256:(i + 1) * 256].rearrange("p (a b) -> p a b", b=64) for i in range(4)],
                    Xn[:, 1024:1536].rearrange("p (h t d) -> p h t d", t=2, d=64), 'x')
            for t0 in range(0, NS, 4):
                subs = list(range(min(4, NS - t0)))
                fence(RETK, XK, eng='dve')
                for s in subs:
                    g = t0 + s
                    load_x(s, 128, xpre[g * 128:(g + 1) * 128, :])
                    norm_T(s, 128, gmixT, 'gmixT')
                    pump(5)
                fence(XK, RETK, eng='dve')
                nsub = len(subs)
                nt = nsub * 128
                for wt in (7, 8):
                    Wv, kv_ = Wres[wt]
                    for s in subs:
                        def ev(b, s=s, wt=wt):
                            P.op('act', lambda e: e.activation(out=vr[:, s, (wt - 7) * 512:(wt - 6) * 512],
                                                               in_=PS[:, b * 512:(b + 1) * 512], func=AF.Copy),
                                 r=[pk(b)], w=['vr'])
                        proj_a(Wv, kv_, 0, 512, s, 128, ev)
                for wt in (5, 6):
                    Wv, kv_ = Wres[wt]
                    hq = wt - 5
                    for s in subs:
                        g = t0 + s
                        cs, cskey = load_cs(cs_pre[g * 128:(g + 1) * 128, :])
                        def ev(b, s=s, hq=hq, cs=cs, cskey=cskey):
                            dst = kt[:, s, hq * 512:(hq + 1) * 512].rearrange("p (h d) -> p h d", d=128)
                            rotary(b, 128, cs, cskey, dk[:, hq * 4:(hq + 1) * 4], 'dk', dst, ('kt', s, hq), peng='dve', scr=xscr)
                        proj_a(Wv, kv_, 0, 512, s, 128, ev)
                if t0 + nsub == NS:
                    s = nsub - 1
                    W2, k2 = loadw('win', 2)
                    for blk in range(2):
                        b = fb()
                        for kc in range(KC):
                            P.op('pe', lambda e, kc=kc, b=b, blk=blk, s=s, W2=W2: e.matmul(
                                PS[:, b * 512:b * 512 + 128], lhsT=W2[:, kc, blk * 128:(blk + 1) * 128],
                                rhs=uTb[:, kc, s * 128:(s + 1) * 128], start=(kc == 0), stop=(kc == KC - 1)),
                                r=[k2, ('uT', s)], w=[pk(b)])
                        P.op('act', lambda e, b=b, blk=blk: e.activation(out=ksT[:, blk, 0:128], in_=PS[:, b * 512:b * 512 + 128],
                                                                         func=AF.Copy), r=[pk(b)], w=['ksT'])
                    def ev(b):
                        P.op('act', lambda e: e.activation(out=vstok[:, 0, :], in_=PS[:, b * 512:b * 512 + 256],
                                                           func=AF.Copy), r=[pk(b)], w=['vs0'])
                    proj_a(W2, k2, 256, 256, s, 128, ev)
                for s in subs:
                    ret_state_update(s, 128, g128, 'g128')
            fence(WRK + XSK, MIXK + FFNK + ['Xn'], eng='dve')

        def sample_setup():
            fence(RETK + FFNK, XK + MIXK)
            P.op('sp', lambda e: e.dma_start(out=Sf[:].rearrange("p a b -> p (a b)"), in_=sret), w=['Sf'], dma=True)
            P.op('act', lambda e: e.activation(out=Sbb[:], in_=Sf[:], func=AF.Copy), r=['Sf'], w=['Sbb'])
            P.op('sp', lambda e: e.dma_start(out=carry[:].rearrange("p a b -> p (a b)"), in_=sconv), w=['carry'], dma=True)
            P.op('sp', lambda e: e.dma_start(out=kvst[0][:, :], in_=ck), w=['kvst0'], dma=True)
            P.op('sp', lambda e: e.dma_start(out=kvst[1][:, :], in_=cv), w=['kvst1'], dma=True)
            P.op('dve', lambda e: e.tensor_copy(out=otok[:, 0:256], in_=kvst[0][:, :]), r=['kvst0'], w=['otok0'])
            P.op('dve', lambda e: e.tensor_copy(out=vstok[:, 0, :], in_=kvst[1][:, :]), r=['kvst1'], w=['vs0'])
            t, tv = transpose_tok(otok, 'otok0', 128, 2, None, None)
            P.op('act', lambda e: e.activation(out=ksT[:, :, 0:128], in_=tv[:, 0:2, :], func=AF.Copy), r=[pk(t)], w=['ksT'])
            P.op('pool', lambda e: e.dma_start(out=ks_o[0:64, :], in_=ck[64:128, :]), dma=True)
            P.op('pool', lambda e: e.dma_start(out=vs_o[0:64, :], in_=cv[64:128, :]), dma=True)
            for s in range(2):
                P.op('sp', lambda e, s=s: e.dma_start(out=Xb[:, s, 0:1024], in_=cmk[s * 128:(s + 1) * 128, :]),
                     w=[('X', s)], dma=True)
                P.op('sp', lambda e, s=s: e.dma_start(out=Xb[:, 2 + s, 0:1024], in_=cmv[s * 128:(s + 1) * 128, :]),
                     w=[('X', 2 + s)], dma=True)
                P.op('dve', lambda e, s=s: e.tensor_copy(out=mvtok[:, s, :], in_=Xb[:, 2 + s, 0:1024]),
                     r=[('X', 2 + s)], w=['mvtok'])
                P.op('dve', lambda e, s=s: e.tensor_copy(out=otok[:, :], in_=Xb[:, s, 0:1024]), r=[('X', s)], w=['otok0'])
                t, tv = transpose_tok(otok, 'otok0', 128, 8, None, None)
                P.op('act', lambda e, s=s, tv=tv: e.activation(out=mkT[:, :, s * 128:(s + 1) * 128], in_=tv[:, :, :],
                                                               func=AF.Copy), r=[pk(t)], w=['mkT'])

        import os as _os
        _st = _os.environ.get("KSTAGES", "mem,prefix,boundary,main,sample").split(",")
        _nmain = int(_os.environ.get("KNMAIN", "8"))
        if 'prefix' in _st:
            prefix_pass()
        if 'mem' in _st:
            mem_pass()
        if 'boundary' in _st:
            full_tile(xpre[3968:4096, :], cs_pre[3968:4096, :], 1, 128, do_down=False)
            P.op('dve', lambda e: e.tensor_scalar(out=carry[:].rearrange("p a b -> p (a b)"),
                                                  in0=carry[:].rearrange("p a b -> p (a b)"),
                                                  scalar1=hflag[:, 0:1], scalar2=None, op0=ALU.mult),
                 r=['carry', 'hflag'], w=['carry'])
        if 'main' in _st:
            for ti in range(_nmain):
                last = (ti == 7)
                if ti + 1 < _nmain:
                    nxt = (xp[(ti + 1) * 512:(ti + 2) * 512, :], 4, 128)
                elif 'sample' in _st:
                    nxt = (xs, 1, 64)
                else:
                    nxt = None
                full_tile(xp[ti * 512:(ti + 1) * 512, :], cs_main[ti * 512:(ti + 1) * 512, :], 4, 128,
                          first_mask=(ti == 0), y_dst=y_o[ti * 512:(ti + 1) * 512, :],
                          k_out=kp_o if last else None, v_out=vp_o if last else None,
                          prenormed=(ti > 0), next_pre=nxt, xq=('pool' if ti > 0 else 'sp'))
        P.op('pool', lambda e: e.dma_start(out=sp_o, in_=Sf[:].rearrange("p a b -> p (a b)")), r=['Sf'], dma=True)
        P.op('pool', lambda e: e.dma_start(out=cp_o, in_=carry[:].rearrange("p a b -> p (a b)")), r=['carry'], dma=True)
        if 'sample' in _st:
            sample_setup()
            full_tile(xs, cs_s, 1, 64, sample=True, y_dst=ys_o, k_out=ks_o[64:128, :], v_out=vs_o[64:128, :],
                      prenormed=('main' in _st))
        P.op('pool', lambda e: e.dma_start(out=ss_o, in_=Sf[:].rearrange("p a b -> p (a b)")), r=['Sf'], dma=True)
        P.op('pool', lambda e: e.dma_start(out=cs_o, in_=carry[:].rearrange("p a b -> p (a b)")), r=['carry'], dma=True)

        P.emit(nc, sems)
    return nc


_CACHE = {}


def _tables():
    half = 64
    inv_freq = (1.0 / (np.float32(10000.0) ** np.linspace(0.0, 1.0, half, dtype=np.float32))).astype(np.float32)

    def cs(pos):
        ang = pos.astype(np.float32)[:, None] * inv_freq[None, :]
        return np.concatenate([np.cos(ang), np.sin(ang)], axis=1).astype(np.float32)

    h = np.arange(8, dtype=np.float64)
    log_g = np.log1p(-np.exp2(-5.0 - h))
    n = np.arange(128, dtype=np.float64)
    dq = np.exp(log_g[None, :] * (n[:, None] + 1.0)).astype(np.float32)
    dk = (np.exp(-log_g[None, :] * (n[:, None] + 1.0)) * (128.0 ** -0.5)).astype(np.float32)
    g128 = np.exp(log_g * 128.0).astype(np.float32)
    g64 = np.exp(log_g * 64.0).astype(np.float32)
    m = np.arange(128)
    caus = (m[None, :] >= m[:, None]).astype(np.float32)
    sel = np.zeros((2, 128), np.float32)
    sel[0, :64] = 1.0
    sel[1, 64:] = 1.0
    mrow = np.zeros((2, 256), np.float32)
    mrow[0, 192:] = NEGM
    mrow[1, :64] = NEGM
    mstd = np.concatenate([mrow, mrow], axis=1)
    mrow0 = mrow.copy()
    mrow0[:, :128] = NEGM
    m0 = np.concatenate([mrow0, mrow0], axis=1)
    return cs, dq, dk, g128, g64, caus, sel, mstd, m0


def kernel(x_prompt, x_sample, mem_prompt, cache_swa_k, cache_swa_v, state_ret, state_ffn_conv,
           cache_mem_k, cache_mem_v, g_mix, w_in, b_gate, sink, w_br, w_o, g_mem, w_mem_kv,
           g_ffn, w_up, w_conv, b_conv, w_down, g_final):
    f = lambda a: np.ascontiguousarray(np.asarray(a, dtype=np.float32))
    x_prompt = f(x_prompt); x_sample = f(x_sample); mem_prompt = f(mem_prompt)
    if 'nc' not in _CACHE:
        _CACHE['nc'] = build_program()
    nc = _CACHE['nc']
    cs, dq, dk, g128, g64, caus, sel, mstd, m0 = _tables()

    def featT(v, nblk):
        return np.ascontiguousarray(f(v).reshape(nblk, 128).T)

    sk = f(sink)[0]
    sinkP = np.zeros(16, np.float32)
    sinkP[:] = sk
    wc = f(w_conv)[0]
    wcT = np.ascontiguousarray(wc.reshape(3, 88, 128).transpose(2, 1, 0)).reshape(128, 264)
    w_in_l = f(w_in)[0].copy()
    for t_ in range(2):
        blk_ = w_in_l[:, t_ * 512:(t_ + 1) * 512].reshape(D, 2, 4, 64)
        w_in_l[:, t_ * 512:(t_ + 1) * 512] = np.ascontiguousarray(blk_.transpose(0, 2, 1, 3)).reshape(D, 512)
    shared = {
        "w_in": w_in_l, "w_br": f(w_br)[0], "w_o": f(w_o)[0], "w_mkv": f(w_mem_kv)[0],
        "w_up": f(w_up)[0], "w_dn": f(w_down)[0],
        "gmixT": featT(f(g_mix)[0], 16), "gmemT": featT(f(g_mem)[0], 16), "gffnT": featT(f(g_ffn)[0], 16),
        "bgT": featT(f(b_gate)[0], 48), "wcT": wcT, "bcT": featT(f(b_conv)[0], 88),
        "sinkP": sinkP, "gfin": f(g_final),
        "cs_s": cs(1024 + np.arange(64)), "dq": dq, "dk": dk, "g128": g128, "g64": g64,
        "caus": caus, "sel": sel, "mstd": mstd,
    }
    zeros_pre = np.zeros((HALF, D), np.float32)
    in_maps = []
    for c in range(NCORE):
        b, half = c // 2, c % 2
        m = dict(shared)
        m["xp"] = x_prompt[b, half * HALF:(half + 1) * HALF]
        m["xpre"] = x_prompt[b, 0:HALF] if half == 1 else zeros_pre
        m["xs"] = x_sample[c]
        m["mem"] = mem_prompt[b]
        m["ck"] = f(cache_swa_k)[0, c].reshape(128, 256)
        m["cv"] = f(cache_swa_v)[0, c].reshape(128, 256)
        m["sret"] = np.ascontiguousarray(f(state_ret)[0, c].transpose(1, 0, 2)).reshape(128, 1024)
        m["sconv"] = np.ascontiguousarray(f(state_ffn_conv)[0, c].reshape(2, 88, 128).transpose(2, 1, 0)).reshape(128, 176)
        m["cmk"] = f(cache_mem_k)[0, c].reshape(256, 1024)
        m["cmv"] = f(cache_mem_v)[0, c].reshape(256, 1024)
        m["cs_pre"] = cs(np.arange(HALF))
        m["cs_main"] = cs(half * HALF + np.arange(HALF))
        m["m0"] = mstd if half == 1 else m0
        m["hflag"] = np.full((128, 1), float(half), np.float32)
        in_maps.append(m)
    if _CACHE.get('sim_hook') is not None:
        return _CACHE['sim_hook'](nc, in_maps)
    import os as _os3
    if _os3.environ.get('KONE'):
        c1 = int(_os3.environ['KONE'])
        return run_bass_kernel_spmd(nc, [in_maps[c1]], core_ids=[0]).results[0]
    ncr = int(_os3.environ.get('KCORES', NCORE))
    res = run_bass_kernel_spmd(nc, in_maps[:ncr], core_ids=list(range(ncr)))
    R = list(res.results) + [res.results[i % ncr] for i in range(ncr, NCORE)]
    y_prompt = np.stack([np.concatenate([R[2 * b]["y"], R[2 * b + 1]["y"]], axis=0) for b in range(4)])
    y_sample = np.stack([R[c]["ys"] for c in range(8)])
    kp = np.stack([R[2 * b + 1]["kp"].reshape(128, 4, 64) for b in range(4)])[None]
    vp = np.stack([R[2 * b + 1]["vp"].reshape(128, 4, 64) for b in range(4)])[None]
    sp = np.stack([R[2 * b + 1]["sp"].reshape(128, 8, 128).transpose(1, 0, 2) for b in range(4)])[None]
    cp = np.stack([R[2 * b + 1]["cp"].reshape(128, 88, 2).transpose(2, 1, 0).reshape(2, 11264) for b in range(4)])[None]
    mk = np.stack([R[2 * b]["mk"].reshape(256, 4, 256) for b in range(4)])[None]
    mv = np.stack([R[2 * b]["mv"].reshape(256, 4, 256) for b in range(4)])[None]
    ks = np.stack([R[c]["ks"].reshape(128, 4, 64) for c in range(8)])[None]
    vs = np.stack([R[c]["vs"].reshape(128, 4, 64) for c in range(8)])[None]
    ss = np.stack([R[c]["ss"].reshape(128, 8, 128).transpose(1, 0, 2) for c in range(8)])[None]
    cs_ = np.stack([R[c]["cs"].reshape(128, 88, 2).transpose(2, 1, 0).reshape(2, 11264) for c in range(8)])[None]
    outs = (y_prompt, y_sample, kp, vp, sp, cp, mk, mv, ks, vs, ss, cs_)
    return tuple(np.ascontiguousarray(o, dtype=np.float32) for o in outs)
```

```python
import numpy as np
from contextlib import ExitStack
import concourse.bass as bass
import concourse.mybir as mybir
from concourse.bass_utils import run_bass_kernel_spmd

F32 = mybir.dt.float32
BF16 = mybir.dt.bfloat16
AF = mybir.ActivationFunctionType
ALU = mybir.AluOpType
AX = mybir.AxisListType

NDMASEM = 12
NDMAQ = {'sp': 8, 'pool': 4, 'act': 4}
D = 2048
KC = 16
NCORE = 8
HALF = 4096
EPS = 1e-6
NEGM = -30000.0
DFF = 5632


class Prog:
    def __init__(self):
        self.ops = []
        self.last_w = {}
        self.rd_eng = {}
        self.rd_dma = {}

    def op(self, eng, fn, r=(), w=(), dma=False):
        idx = len(self.ops)
        deps = set()
        pr = [k for k in r if isinstance(k, tuple) and k[0] == 'ps']
        if pr:
            r = [k for k in r if not (isinstance(k, tuple) and k[0] == 'ps')]
            w = list(w) + [k for k in pr if k not in w]
        for k in r:
            d = self.last_w.get(k)
            if d is not None:
                deps.add(d)
        for k in w:
            d = self.last_w.get(k)
            if d is not None:
                deps.add(d)
            for x in self.rd_eng.get(k, {}).values():
                deps.add(x)
            for x in self.rd_dma.get(k, ()):
                deps.add(x)
        for k in r:
            if dma:
                self.rd_dma.setdefault(k, []).append(idx)
            else:
                self.rd_eng.setdefault(k, {})[eng] = idx
        for k in w:
            self.last_w[k] = idx
            self.rd_eng[k] = {}
            self.rd_dma[k] = []
        deps.discard(idx)
        self.ops.append(dict(eng=eng, fn=fn, deps=deps, dma=dma))
        return idx

    def emit(self, nc, sems):
        ops = self.ops
        for o in ops:
            o['sig'] = o['dma']
        for o in ops:
            for d in o['deps']:
                od = ops[d]
                if od['eng'] == 'pe' and o['eng'] == 'pe' and not od['dma'] and not o['dma']:
                    continue
                od['sig'] = True
        seq = {'pe': 0, 'act': 0, 'dve': 0, 'pool': 0}
        dcount = {}
        dma_n = {'sp': 0, 'pool': 0, 'act': 0}
        for o in ops:
            if o['dma']:
                q = o['eng']
                j = dma_n[q] % NDMAQ[q]
                dma_n[q] += 1
                key = (q, j)
                dcount[key] = dcount.get(key, 0) + 1
                o['sem'] = key
                o['val'] = 16 * dcount[key]
            elif o['sig']:
                seq[o['eng']] += 1
                o['sem'] = o['eng']
                o['val'] = seq[o['eng']]
        streams = {e: [] for e in ('pe', 'act', 'dve', 'pool', 'sp')}
        for i, o in enumerate(ops):
            streams[o['eng']].append(i)
        final_dma = dict(dcount)
        self.stats = dict(seq=dict(seq), dma=dict(dma_n), nops=len(ops))
        print('PROG stats', self.stats, flush=True)

        def run_stream(eng_name, eng):
            waited = {}
            for i in streams[eng_name]:
                o = ops[i]
                need = {}
                for d in o['deps']:
                    od = ops[d]
                    if (od['eng'] == 'pe' and eng_name == 'pe'
                            and not od['dma'] and not o['dma']):
                        continue
                    s, v = od['sem'], od['val']
                    if need.get(s, 0) < v:
                        need[s] = v
                if o['dma'] and o['val'] > 16:
                    s = o['sem']
                    if need.get(s, 0) < o['val'] - 16:
                        need[s] = o['val'] - 16
                for s, v in need.items():
                    if waited.get(s, 0) < v:
                        eng.wait_ge(sems[s], v)
                        waited[s] = v
                ins = o['fn'](eng)
                if o['dma']:
                    ins.then_inc(sems[o['sem']], 16)
                elif o['sig']:
                    ins.then_inc(sems[o['sem']], 1)
            if eng_name == 'sp':
                for s, c in final_dma.items():
                    eng.wait_ge(sems[s], 16 * c)

        with nc.Block() as block:
            @block.sync
            def _(e):
                run_stream('sp', e)

            @block.tensor
            def _(e):
                run_stream('pe', e)

            @block.scalar
            def _(e):
                run_stream('act', e)

            @block.vector
            def _(e):
                run_stream('dve', e)

            @block.gpsimd
            def _(e):
                run_stream('pool', e)


def build_program():
    nc = bass.Bass("TRN2", target_bir_lowering=False)
    P = Prog()

    def din(name, shape):
        return nc.dram_tensor(name, list(shape), F32, kind="ExternalInput").ap()

    def dout(name, shape):
        return nc.dram_tensor(name, list(shape), F32, kind="ExternalOutput").ap()

    def dscr(name, shape):
        return nc.dram_tensor(name, list(shape), BF16, kind="Internal").ap()

    xp = din("xp", [HALF, D]); xpre = din("xpre", [HALF, D]); xs = din("xs", [64, D])
    mem = din("mem", [256, D])
    ck = din("ck", [128, 256]); cv = din("cv", [128, 256])
    sret = din("sret", [128, 1024]); sconv = din("sconv", [128, 176])
    cmk = din("cmk", [256, 1024]); cmv = din("cmv", [256, 1024])
    w_in = din("w_in", [D, 12800]); w_br = din("w_br", [3072, D]); w_o = din("w_o", [D, D])
    w_mkv = din("w_mkv", [D, D]); w_up = din("w_up", [D, 2 * DFF]); w_dn = din("w_dn", [DFF, D])
    gmixT_d = din("gmixT", [128, 16]); gmemT_d = din("gmemT", [128, 16]); gffnT_d = din("gffnT", [128, 16])
    bgT_d = din("bgT", [128, 48]); wcT_d = din("wcT", [128, 264]); bcT_d = din("bcT", [128, 88])
    sinkP_d = din("sinkP", [16]); gfin_d = din("gfin", [D])
    cs_pre = din("cs_pre", [HALF, 128]); cs_main = din("cs_main", [HALF, 128]); cs_s = din("cs_s", [64, 128])
    dq_d = din("dq", [128, 8]); dk_d = din("dk", [128, 8]); g128_d = din("g128", [8]); g64_d = din("g64", [8])
    caus_d = din("caus", [128, 128]); sel_d = din("sel", [2, 128])
    mstd_d = din("mstd", [2, 512]); m0_d = din("m0", [2, 512]); hflag_d = din("hflag", [128, 1])

    y_o = dout("y", [HALF, D]); ys_o = dout("ys", [64, D])
    kp_o = dout("kp", [128, 256]); vp_o = dout("vp", [128, 256])
    sp_o = dout("sp", [128, 1024]); cp_o = dout("cp", [128, 176])
    mk_o = dout("mk", [256, 1024]); mv_o = dout("mv", [256, 1024])
    ks_o = dout("ks", [128, 256]); vs_o = dout("vs", [128, 256])
    ss_o = dout("ss", [128, 1024]); cs_o = dout("cs", [128, 176])

    s_win = dscr("s_win", [25, 128, 8192]); s_mkv = dscr("s_mkv", [4, 128, 8192])
    s_wbr = dscr("s_wbr", [12, 128, 4096]); s_wo = dscr("s_wo", [4, 128, 8192])
    s_wup = dscr("s_wup", [22, 128, 8192]); s_wdn = dscr("s_wdn", [16, 128, 5632])

    with ExitStack() as es:
        def sb(name, shape, dt):
            return es.enter_context(nc.sbuf_tensor(name, list(shape), dt))

        PS = es.enter_context(nc.psum_tensor("PS", [128, 4096], F32))
        PSB = PS[:, :].bitcast(BF16)

        sems = {}
        for n in ('pe', 'act', 'dve', 'pool'):
            sems[n] = es.enter_context(nc.semaphore("s_" + n))
        for q in ('sp', 'pool'):
            for i in range(NDMASEM):
                sems[(q, i)] = es.enter_context(nc.semaphore("d_%s%d" % (q, i)))

        Xb = sb("Xb", [128, 4, D], F32)
        uTb = sb("uTb", [128, KC, 512], BF16)
        Wp = [sb("Wp%d" % i, [128, 8192], BF16) for i in range(3)]
        Ub = sb("Ub", [128, D], BF16)
        Xn = sb("Xn", [128, D], F32)
        ident = sb("ident", [128, 128], BF16)
        identf = sb("identf", [128, 128], F32)
        gmixT = sb("gmixT_s", [128, 16], F32); gmemT = sb("gmemT_s", [128, 16], F32)
        gffnT = sb("gffnT_s", [128, 16], F32)
        bgT = sb("bgT_s", [128, 48], F32); wcT = sb("wcT_s", [128, 88, 3], F32); bcT = sb("bcT_s", [128, 88], F32)
        sinkb = sb("sinkb", [128, 16], F32)
        dq = sb("dq_s", [128, 8], F32); dk = sb("dk_s", [128, 8], F32)
        g128 = sb("g128_s", [128, 8], F32); g64 = sb("g64_s", [128, 8], F32)
        caus = sb("caus_s", [128, 128], F32)
        mskf = sb("mskf", [128, 1152], F32)
        mskb = sb("mskb", [128, 1152], BF16)
        sel = mskb[:, 0:128]; mstd = mskb[:, 128:640]; m0 = mskb[:, 640:1152]
        ksT = sb("ksT", [128, 2, 640], BF16); vstok = sb("vstok", [128, 5, 256], BF16)
        mkT = sb("mkT", [128, 8, 256], BF16); mvtok = sb("mvtok", [128, 2, 1024], BF16)
        Sf = sb("Sf", [128, 8, 128], F32); Sbb = sb("Sbb", [128, 8, 128], BF16)
        carry = sb("carry", [128, 88, 2], F32)
        cst = [sb("cst%d" % i, [128, 128], F32) for i in range(2)]
        st = sb("stats", [128, 96], F32)
        stp = sb("stats2", [128, 2, 48], F32)
        rinv2 = sb("rinv2", [128, 2, 16], F32)
        fdummy = sb("fdummy", [128, 2], F32)
        hflag = sb("hflag_s", [128, 1], F32)
        ss_ = st[:, 0:1]; std_ = st[:, 1:2]; rstd_ = st[:, 2:3]
        mx8 = st[:, 8:16]; nmx8 = st[:, 16:24]; rs8 = st[:, 24:32]; es8 = st[:, 32:40]
        rinvN = st[:, 40:56]; ssq8 = st[:, 56:64]; std8 = st[:, 64:72]; rstd8 = st[:, 72:80]; den8 = st[:, 80:88]

        OVN = 34 * 1024 + 64
        OV = sb("OV", [128, OVN], BF16)
        ovpos = [0]

        def carve(nelem_bf16):
            a = ovpos[0]
            ovpos[0] += nelem_bf16
            assert ovpos[0] <= OVN, ovpos[0]
            return OV[:, a:a + nelem_bf16]

        QT = carve(4096).rearrange("p (a b) -> p a b", b=512)
        obT = carve(4096).rearrange("p (a b) -> p a b", b=512)
        mT = carve(8192).rearrange("p (a b) -> p a b", b=512)
        Pb = carve(2048).rearrange("p (a b) -> p a b", b=256)
        PTs = carve(2048).rearrange("p (a b) -> p a b", b=128)
        otok = carve(1024)
        otok_b = carve(1024)
        otok2 = [otok, otok_b]
        qtok_c = carve(512).rearrange("p (a b) -> p a b", b=128)
        qtok_d = carve(512).rearrange("p (a b) -> p a b", b=128)
        rt = [carve(512).bitcast(F32).rearrange("p (a b) -> p a b", b=64) for _ in range(4)]
        qf = carve(1024).bitcast(F32).rearrange("p (h t d) -> p h t d", t=2, d=64)
        qtok = carve(512).rearrange("p (a b) -> p a b", b=128)
        qtok_b = carve(512).rearrange("p (a b) -> p a b", b=128)
        qtok2 = [qtok, qtok_b, qtok_c, qtok_d]
        gsb = [carve(512) for _ in range(2)]
        ytmp = [carve(512) for _ in range(2)]
        kvst = [carve(512).bitcast(F32) for _ in range(2)]
        mixer_end = ovpos[0]
        ovpos[0] = 0
        hT = carve(44 * 512).rearrange("p (a b) -> p a b", b=512)
        abuf = [[carve(1040).bitcast(F32) for _ in range(2)] for _ in range(2)]
        ctmp = [carve(1024).bitcast(F32) for _ in range(2)]
        sgt = carve(1024).bitcast(F32)
        sgt_b = carve(1024).bitcast(F32)
        sgt2 = [sgt, sgt_b]
        gfin = carve(4096).bitcast(F32)
        ffn_end = ovpos[0]
        XbB = Xb[:, :, :].rearrange("p a b -> p (a b)").bitcast(BF16)
        krT = XbB[:, 0:4096].rearrange("p (a b) -> p a b", b=512)
        vr = XbB[:, 4096:8192].rearrange("p (a b) -> p a b", b=1024)
        gr = XbB[:, 8192:12288].rearrange("p (a b) -> p a b", b=1024)
        kt = XbB[:, 12288:16384].rearrange("p (a b) -> p a b", b=1024)

        MIXK = [('Pb', 0), ('Pb', 1), ('PTs', 0), ('PTs', 1), ('AT', 0), ('AT', 1), 'QT', 'obT', 'mT', 'Pb', 'PTs', 'otok', 'rt', 'qf', 'qtok', 'qtok0', 'qtok1', 'qtok2', 'qtok3', 'otok0', 'otok1', 'gsb0', 'gsb1', 'ytmp0', 'ytmp1',
                'kvst0', 'kvst1']
        FFNK = ['hT', 'abuf00', 'abuf01', 'abuf10', 'abuf11', 'ctmp0', 'ctmp1', 'sgt', 'sgt0', 'sgt1', 'gfin']
        XK = [('X', s) for s in range(4)]
        RETK = ['krT', 'vr', 'gr'] + [('kt', s_, hq_) for s_ in range(4) for hq_ in range(2)]

        def fence(old, new, eng='pool'):
            P.op(eng, lambda e: e.memset(fdummy[:, 0:1], 0.0), r=[], w=['fdummy'] + list(old) + list(new))

        def bank(i):
            return PS[:, i * 512:(i + 1) * 512]

        def tbank(i):
            return PSB[:, i * 1024:(i + 1) * 1024].rearrange("p (a b) -> p a b", b=128)

        def pk(i):
            return ('ps', i)

        rot = {'f': 0, 't': 0, 'w': 0, 'cs': 0, 'q': 0}

        def fb():
            i = rot['f'] % 6
            rot['f'] += 1
            return i

        def tb():
            i = 6 + rot['t'] % 2
            rot['t'] += 1
            return i

        def ld(dst, src, key, q='sp'):
            P.op(q, lambda e: e.dma_start(out=dst, in_=src), w=[key], dma=True)

        P.op('pool', lambda e: e.memset(identf[:], 1.0), w=['identf'])
        P.op('pool', lambda e: e.affine_select(out=identf[:], in_=identf[:], pattern=[[-1, 128]],
                                               compare_op=ALU.is_equal, fill=0.0, base=0,
                                               channel_multiplier=1), r=['identf'], w=['identf'])
        P.op('dve', lambda e: e.tensor_copy(out=ident[:], in_=identf[:]), r=['identf'], w=['ident'])
        ld(gmixT[:], gmixT_d, 'gmixT'); ld(gmemT[:], gmemT_d, 'gmemT'); ld(gffnT[:], gffnT_d, 'gffnT')
        ld(bgT[:], bgT_d, 'bgT'); ld(wcT[:].rearrange("p a b -> p (a b)"), wcT_d, 'wcT'); ld(bcT[:], bcT_d, 'bcT')
        ld(sinkb[:], sinkP_d.partition_broadcast(128), 'sinkb')
        ld(dq[:], dq_d, 'dq'); ld(dk[:], dk_d, 'dk')
        ld(g128[:], g128_d.partition_broadcast(128), 'g128'); ld(g64[:], g64_d.partition_broadcast(128), 'g64')
        ld(caus[:], caus_d, 'caus')
        ld(hflag[:], hflag_d, 'hflag')
        P.op('pool', lambda e: e.memset(mskf[:], 0.0), w=['mskf'])
        for pb_ in (0, 64):
            P.op('sp', lambda e, pb_=pb_: e.dma_start(out=mskf[pb_:pb_ + 2, 0:128], in_=sel_d), w=['mskf'], dma=True)
            P.op('sp', lambda e, pb_=pb_: e.dma_start(out=mskf[pb_:pb_ + 2, 128:640], in_=mstd_d), w=['mskf'], dma=True)
            P.op('sp', lambda e, pb_=pb_: e.dma_start(out=mskf[pb_:pb_ + 2, 640:1152], in_=m0_d), w=['mskf'], dma=True)
        P.op('dve', lambda e: e.tensor_copy(out=mskb[:], in_=mskf[:]), r=['mskf'], w=['sel', 'mstd', 'm0'])
        P.op('pool', lambda e: e.memset(Sf[:], 0.0), w=['Sf'])
        P.op('pool', lambda e: e.memset(Sbb[:], 0.0), w=['Sbb'])
        P.op('pool', lambda e: e.memset(carry[:], 0.0), w=['carry'])
        P.op('pool', lambda e: e.memset(ksT[:], 0.0), w=['ksT'])
        P.op('pool', lambda e: e.memset(vstok[:], 0.0), w=['vs0', 'vs1', 'vs2', 'vs3', 'vs4'])

        def conv_w(dst_tile, src, kcn, key):
            P.op('pool', lambda e: e.dma_start(
                out=dst_tile.rearrange("p (kc c) -> p kc c", c=512),
                in_=src.rearrange("(kc p) c -> p kc c", p=128)), w=[key], dma=True)

        conv_done = set()

        def conv_tile(name, t):
            if (name, t) in conv_done:
                return
            conv_done.add((name, t))
            if name == 'mkv':
                conv_w(s_mkv[t], w_mkv[:, t * 512:(t + 1) * 512], 16, ('s', 'mkv', t))
            elif name == 'win':
                if True:
                    conv_w(s_win[t], w_in[:, t * 512:(t + 1) * 512], 16, ('s', 'win', t))
            elif name == 'wbr':
                b_, cg = t // 4, t % 4
                conv_w(s_wbr[t], w_br[b_ * 1024:(b_ + 1) * 1024, cg * 512:(cg + 1) * 512], 8, ('s', 'wbr', t))
            elif name == 'wo':
                conv_w(s_wo[t], w_o[:, t * 512:(t + 1) * 512], 16, ('s', 'wo', t))
            elif name == 'wup':
                conv_w(s_wup[t], w_up[:, t * 512:(t + 1) * 512], 16, ('s', 'wup', t))
            elif name == 'wdn':
                kg, cg = t // 4, t % 4
                conv_w(s_wdn[t], w_dn[kg * 1408:(kg + 1) * 1408, cg * 512:(cg + 1) * 512], 11, ('s', 'wdn', t))

        conv_order = [('win', t) for t in (7, 8, 5, 6, 2)] + [('mkv', t) for t in range(4)] + [('win', t) for t in (0, 1)]
        conv_order += [('win', 13 + i) for i in range(4)] + [('wbr', i) for i in range(4)]
        conv_order += [('win', t) for t in (9, 10, 3, 4)]
        conv_order += [('win', 17 + i) for i in range(4)] + [('wbr', 4 + i) for i in range(4)]
        conv_order += [('win', t) for t in (11, 12)]
        conv_order += [('win', 21 + i) for i in range(4)] + [('wbr', 8 + i) for i in range(4)]
        conv_order += [('wo', t) for t in range(4)]
        for i in range(11):
            conv_order += [('wup', i), ('wup', 11 + i)]
        conv_late = []
        for cg in range(4):
            for kg in range(4):
                conv_late.append(('wdn', kg * 4 + cg))

        def pump(n):
            while n > 0 and conv_order:
                name, t = conv_order.pop(0)
                if (name, t) not in conv_done:
                    conv_tile(name, t)
                    n -= 1

        def skeys(name, t):
            return [('s', name, t)]

        SCR = {'win': s_win, 'mkv': s_mkv, 'wbr': s_wbr, 'wo': s_wo, 'wup': s_wup, 'wdn': s_wdn}
        KCN = {'win': 16, 'mkv': 16, 'wbr': 8, 'wo': 16, 'wup': 16, 'wdn': 11}

        def loadw(name, t):
            conv_tile(name, t)
            j = rot['w'] % 3
            rot['w'] += 1
            n = KCN[name] * 512
            src = SCR[name][t]
            P.op('sp', lambda e: e.dma_start(out=Wp[j][:, 0:n], in_=src), r=skeys(name, t), w=[('W', j)], dma=True)
            return Wp[j][:, 0:n].rearrange("p (kc c) -> p kc c", c=512), ('W', j)

        def norm_part(s, np_, X, xkey):
            P.op('act', lambda e: e.activation(out=Ub[0:np_, :], in_=X, func=AF.Square, accum_out=ss_[0:np_]),
                 r=[xkey], w=['U', 'ss'])
            P.op('act', lambda e: e.activation(out=std_[0:np_], in_=ss_[0:np_], func=AF.Sqrt, scale=1.0 / D, bias=EPS),
                 r=['ss'], w=['std'])
            P.op('dve', lambda e: e.reciprocal(out=rstd_[0:np_], in_=std_[0:np_]), r=['std'], w=['rstd'])
            P.op('dve', lambda e: e.tensor_scalar(out=Ub[0:np_, :], in0=X, scalar1=rstd_[0:np_, 0:1], scalar2=None,
                                                  op0=ALU.mult), r=[xkey, 'rstd'], w=['U'])

        def T_part(s, np_, gT, gkey):
            for hb in range(2):
                t = tb()
                tv = tbank(t)
                for k in range(8):
                    kc = hb * 8 + k
                    P.op('pe', lambda e, k=k, kc=kc, tv=tv: e.transpose(
                        out=tv[:, k, 0:np_], in_=Ub[0:np_, kc * 128:(kc + 1) * 128], identity=ident[0:np_, 0:np_]),
                        r=['U', 'ident'], w=[pk(t)])
                P.op('dve', lambda e, hb=hb, tv=tv: e.tensor_tensor(
                    out=uTb[:, hb * 8:(hb + 1) * 8, s * 128:s * 128 + np_], in0=tv[:, :, 0:np_],
                    in1=gT[:, hb * 8:(hb + 1) * 8].unsqueeze(2).broadcast_to([128, 8, np_]), op=ALU.mult),
                    r=[pk(t), gkey], w=[('uT', s)])

        def norm_T(s, np_, gT, gkey):
            norm_part(s, np_, Xb[0:np_, s, :], ('X', s))
            T_part(s, np_, gT, gkey)

        def load_x(s, np_, src, q='sp'):
            P.op(q, lambda e: e.dma_start(out=Xb[0:np_, s, :], in_=src), w=[('X', s)], dma=True)

        def proj_b(Wv, wkey, c0, nsub, nt, evac, mrows=128):
            b = fb()
            for kc in range(KC):
                P.op('pe', lambda e, kc=kc: e.matmul(PS[0:mrows, b * 512:b * 512 + nt], lhsT=Wv[:, kc, c0:c0 + mrows],
                                                     rhs=uTb[:, kc, 0:nt], start=(kc == 0), stop=(kc == KC - 1)),
                     r=[wkey] + [('uT', s) for s in range(nsub)], w=[pk(b)])
            evac(b)

        def proj_a(Wv, wkey, c0, ncols, s, np_, evac):
            b = fb()
            for kc in range(KC):
                P.op('pe', lambda e, kc=kc: e.matmul(PS[0:np_, b * 512:b * 512 + ncols],
                                                     lhsT=uTb[:, kc, s * 128:s * 128 + np_],
                                                     rhs=Wv[:, kc, c0:c0 + ncols], start=(kc == 0), stop=(kc == KC - 1)),
                     r=[wkey, ('uT', s)], w=[pk(b)])
            evac(b)

        def transpose_tok(src, srckey, np_, nblk, dst_fn, dstkey, eng='act'):
            t = tb()
            tv = tbank(t)
            for k in range(nblk):
                P.op('pe', lambda e, k=k: e.transpose(out=tv[:, k, 0:np_], in_=src[0:np_, k * 128:(k + 1) * 128],
                                                      identity=ident[0:np_, 0:np_]),
                     r=[srckey, 'ident'], w=[pk(t)])
            return t, tv

        def rotary(b, np_, cs, cskey, dtab, dkey, out_bf, outkey, peng='pool', scr=None):
            pv = PS[0:np_, b * 512:(b + 1) * 512].rearrange("p (h t d) -> p h t d", t=2, d=64)
            x1 = pv[:, :, 0, :]
            x2 = pv[:, :, 1, :]
            cosb = cs[0:np_, 0:64].unsqueeze(1).broadcast_to([np_, 4, 64])
            sinb = cs[0:np_, 64:128].unsqueeze(1).broadcast_to([np_, 4, 64])
            if scr is None:
                rts, qfv, kp_ = rt, qf, ''
            else:
                rts, qfv, kp_ = scr
            qf_ = qfv
            t1, t2, t3, t4 = [r_[0:np_] for r_ in rts]
            P.op('dve', lambda e: e.tensor_tensor(out=t1, in0=x1, in1=cosb, op=ALU.mult), r=[pk(b), cskey], w=[kp_ + 'rt0'])
            P.op('dve', lambda e: e.tensor_tensor(out=t2, in0=x2, in1=sinb, op=ALU.mult), r=[pk(b), cskey], w=[kp_ + 'rt1'])
            P.op('dve', lambda e: e.tensor_tensor(out=t3, in0=x1, in1=sinb, op=ALU.mult), r=[pk(b), cskey], w=[kp_ + 'rt2'])
            P.op('dve', lambda e: e.tensor_tensor(out=t4, in0=x2, in1=cosb, op=ALU.mult), r=[pk(b), cskey], w=[kp_ + 'rt3'])
            P.op(peng, lambda e: e.tensor_tensor(out=qf_[0:np_, :, 0, :], in0=t1, in1=t2, op=ALU.subtract),
                 r=[kp_ + 'rt0', kp_ + 'rt1'], w=[kp_ + 'qf'])
            P.op(peng, lambda e: e.tensor_tensor(out=qf_[0:np_, :, 1, :], in0=t3, in1=t4, op=ALU.add),
                 r=[kp_ + 'rt2', kp_ + 'rt3'], w=[kp_ + 'qf'])
            P.op(peng, lambda e: e.tensor_tensor(
                out=out_bf, in0=qf_[0:np_].rearrange("p h t d -> p h (t d)"),
                in1=dtab.unsqueeze(2).broadcast_to([np_, 4, 128]), op=ALU.mult),
                r=[kp_ + 'qf', dkey], w=[outkey])

        def ret_dS(s, np_):
            for h in range(8):
                P.op('pe', lambda e, h=h: e.matmul(PS[:, 1024 + h * 128:1024 + (h + 1) * 128],
                                                   lhsT=kt[0:np_, s, h * 128:(h + 1) * 128],
                                                   rhs=vr[0:np_, s, h * 128:(h + 1) * 128], start=True, stop=True),
                     r=[('kt', s, h // 4), 'vr'], w=[pk(2 + h // 4)])

        def ret_state_fin(s, np_, gC, gkey, peng='pool'):
            dS = PS[:, 1024:2048].rearrange("p (h d) -> p h d", d=128)
            gb = gC[:, :].unsqueeze(2).broadcast_to([128, 8, 128])
            P.op('dve', lambda e: e.tensor_tensor(out=Sf[:], in0=dS, in1=Sf[:], op=ALU.add),
                 r=[pk(2), pk(3), 'Sf'], w=['Sf'])
            P.op('dve', lambda e: e.tensor_tensor(out=Sbb[:], in0=Sf[:], in1=gb, op=ALU.mult),
                 r=['Sf', gkey], w=['Sbb'])
            P.op(peng, lambda e: e.tensor_tensor(out=Sf[:], in0=Sf[:], in1=gb, op=ALU.mult),
                 r=['Sf', gkey], w=['Sf'])

        def ret_state_update(s, np_, gC, gkey):
            ret_dS(s, np_)
            ret_state_fin(s, np_, gC, gkey, peng='dve')

        def load_cs(src):
            i = rot['cs'] % 2
            rot['cs'] += 1
            P.op('sp', lambda e: e.dma_start(out=cst[i][0:src.shape[0], :], in_=src), w=[('cs', i)], dma=True)
            return cst[i], ('cs', i)

        def swa_kv_proj(W2, k2, subs, np_, nsub, nt, out_k=None, out_v=None, out_sub=None):
            for blk in range(2):
                def ev(b, blk=blk):
                    P.op('act', lambda e: e.activation(out=ksT[:, blk, 128:128 + nt], in_=PS[:, b * 512:b * 512 + nt],
                                                       func=AF.Copy), r=[pk(b)], w=['ksT'])
                proj_b(W2, k2, blk * 128, nsub, nt, ev)
            for s in subs:
                def ev(b, s=s):
                    P.op('act', lambda e: e.activation(out=vstok[0:np_, 1 + s, :], in_=PS[0:np_, b * 512:b * 512 + 256],
                                                       func=AF.Copy), r=[pk(b)], w=['vs%d' % (1 + s)])
                    if out_v is not None and s == out_sub:
                        i = s % 2
                        P.op('dve', lambda e: e.tensor_copy(out=kvst[i][0:np_, :], in_=PS[0:np_, b * 512:b * 512 + 256]),
                             r=[pk(b)], w=['kvst%d' % i])
                        P.op('pool', lambda e: e.dma_start(out=out_v, in_=kvst[i][0:np_, :]), r=['kvst%d' % i], dma=True)
                proj_a(W2, k2, 256, 256, s, np_, ev)
            if out_k is not None:
                s = out_sub
                def ev(b):
                    P.op('dve', lambda e: e.tensor_copy(out=kvst[1][0:np_, :], in_=PS[0:np_, b * 512:b * 512 + 256]),
                         r=[pk(b)], w=['kvst1'])
                    P.op('pool', lambda e: e.dma_start(out=out_k, in_=kvst[1][0:np_, :]), r=['kvst1'], dma=True)
                proj_a(W2, k2, 0, 256, s, np_, ev)

        def evac_PT(t, tv, n, dst0, np_):
            P.op('act', lambda e: e.activation(out=PTs[:, dst0:dst0 + n, 0:np_], in_=tv[:, 0:n, 0:np_], func=AF.Copy),
                 r=[pk(t)], w=['PTs'])

        def otok_to_obT(s, np_):
            t, tv = transpose_tok(otok, 'otok', np_, 8, None, None)
            P.op('act', lambda e: e.activation(out=obT[:, :, s * 128:s * 128 + np_], in_=tv[:, :, 0:np_], func=AF.Copy),
                 r=[pk(t)], w=['obT'])

        def otok_T(oi, s, np_):
            t, tv = transpose_tok(otok2[oi], 'otok%d' % oi, np_, 8, None, None)
            P.op('act', lambda e: e.activation(out=obT[:, :, s * 128:s * 128 + np_], in_=tv[:, :, 0:np_], func=AF.Copy),
                 r=[pk(t)], w=['obT'])

        def attn_all(kind, subs, np_, nk, mask_for=None):
            nr = 4 if kind == 'swa' else 1
            rounds = [(s, r) for s in subs for r in range(nr)]
            nkh = (nk + 127) // 128
            dh = 64 if kind == 'swa' else 256
            nh = 16 if kind == 'swa' else 4
            deferred = []

            def S(idx):
                s, r = rounds[idx]
                par = idx % 2
                if kind == 'swa':
                    maskb, maskkey = mask_for(s)
                    k0 = s * 128
                    t = r // 2
                    pb = (r % 2) * 64
                    for half in range(2):
                        bk = par * 2 + half
                        P.op('pe', lambda e, bk=bk, pb=pb, maskb=maskb: e.matmul(
                            PS[0:np_, bk * 512:(bk + 1) * 512], lhsT=sel[pb:pb + 2, 0:np_], rhs=maskb[pb:pb + 2, :],
                            start=True, stop=False), r=['sel', maskkey], w=[pk(bk)])
                        for hf in range(2):
                            a = half * 2 + hf
                            P.op('pe', lambda e, bk=bk, pb=pb, hf=hf, a=a, t=t, s=s, k0=k0: e.matmul(
                                PS[0:np_, bk * 512 + hf * 256:bk * 512 + hf * 256 + nk],
                                lhsT=QT[pb:pb + 64, t * 4 + a, s * 128:s * 128 + np_],
                                rhs=ksT[pb:pb + 64, t, k0:k0 + nk], start=False, stop=(hf == 1)),
                                r=['QT', 'ksT'], w=[pk(bk)])
                else:
                    for h in range(4):
                        bk = par * 2 + h // 2
                        for dc in range(2):
                            P.op('pe', lambda e, h=h, dc=dc, bk=bk, s=s: e.matmul(
                                PS[0:np_, bk * 512 + (h % 2) * 256:bk * 512 + (h % 2) * 256 + 256],
                                lhsT=QT[:, h * 2 + dc, s * 128:s * 128 + np_],
                                rhs=mkT[:, h * 2 + dc, :], start=(dc == 0), stop=(dc == 1)),
                                r=['QT', 'mkT'], w=[pk(bk)])

            def SM(idx):
                s, r = rounds[idx]
                par = idx % 2
                base = par * 1024
                S4 = PS[0:np_, base:base + 1024].rearrange("p (h k) -> p h k", k=256)[:, :, 0:nk]
                mx = stp[0:np_, par, 0:4]; nmx = stp[0:np_, par, 4:8]; rs = stp[0:np_, par, 8:12]
                es = stp[0:np_, par, 12:16]; den = stp[0:np_, par, 16:20]
                kk = lambda n: (n, par)
                bks = [pk(par * 2), pk(par * 2 + 1)]
                ri = rinv2[0:np_, s % 2, r * 4:(r + 1) * 4]
                P.op('dve', lambda e: e.tensor_reduce(out=mx, in_=S4, axis=AX.X, op=ALU.max), r=bks, w=[kk('mx')])
                if kind == 'swa':
                    sk = sinkb[0:np_, r * 4:(r + 1) * 4]
                    P.op('dve', lambda e: e.tensor_tensor(out=mx, in0=mx, in1=sk, op=ALU.max),
                         r=[kk('mx'), 'sinkb'], w=[kk('mx')])
                P.op('dve', lambda e: e.tensor_scalar(out=nmx, in0=mx, scalar1=-1.0, scalar2=None, op0=ALU.mult),
                     r=[kk('mx')], w=[kk('nmx')])
                for i in range(4):
                    P.op('act', lambda e, i=i: e.activation(
                        out=Pb[0:np_, par * 4 + i, 0:nk], in_=PS[0:np_, base + i * 256:base + i * 256 + nk],
                        func=AF.Exp, bias=stp[0:np_, par, 4 + i:5 + i], accum_out=stp[0:np_, par, 8 + i:9 + i]),
                        r=[pk(par * 2 + i // 2), kk('nmx')], w=[('Pb', par), kk('rs')])
                if kind == 'swa':
                    P.op('dve', lambda e: e.tensor_tensor(out=es, in0=sk, in1=nmx, op=ALU.add),
                         r=['sinkb', kk('nmx')], w=[kk('es')])
                    P.op('act', lambda e: e.activation(out=es, in_=es, func=AF.Exp), r=[kk('es')], w=[kk('es')])

            def SM2(idx):
                s, r = rounds[idx]
                par = idx % 2
                rs = stp[0:np_, par, 8:12]
                es = stp[0:np_, par, 12:16]; den = stp[0:np_, par, 16:20]
                kk = lambda n: (n, par)
                ri = rinv2[0:np_, s % 2, r * 4:(r + 1) * 4]
                if kind == 'swa':
                    P.op('dve', lambda e: e.tensor_tensor(out=den, in0=rs, in1=es, op=ALU.add),
                         r=[kk('rs'), kk('es')], w=[kk('den')])
                    P.op('dve', lambda e: e.reciprocal(out=ri, in_=den), r=[kk('den')], w=[('rinv', s % 2)])
                else:
                    P.op('dve', lambda e: e.reciprocal(out=ri, in_=rs), r=[kk('rs')], w=[('rinv', s % 2)])

            def TP(idx):
                s, r = rounds[idx]
                par = idx % 2
                tt = tb()
                tv = tbank(tt)
                for i in range(4):
                    for kh in range(nkh):
                        kw = min(128, nk - kh * 128)
                        P.op('pe', lambda e, i=i, kh=kh, kw=kw: e.transpose(
                            out=tv[0:kw, i * 2 + kh, 0:np_], in_=Pb[0:np_, par * 4 + i, kh * 128:kh * 128 + kw],
                            identity=ident[0:np_, 0:np_]), r=[('Pb', par), 'ident'], w=[pk(tt)])
                P.op('act', lambda e: e.activation(out=PTs[:, par * 8:par * 8 + 8, 0:np_], in_=tv[:, 0:8, 0:np_],
                                                   func=AF.Copy), r=[pk(tt)], w=[('PTs', par)])

            def PV(idx):
                s, r = rounds[idx]
                par = idx % 2
                for i in range(4):
                    h = r * 4 + i
                    for kh in range(nkh):
                        kw = min(128, nk - kh * 128)
                        if kind == 'swa':
                            rhs = vstok[0:kw, s + kh, r * 64:(r + 1) * 64]
                            rkey = 'vs%d' % (s + kh)
                        else:
                            rhs = mvtok[:, kh, i * 256:(i + 1) * 256]
                            rkey = 'mvtok'
                        P.op('pe', lambda e, i=i, h=h, kh=kh, kw=kw, rhs=rhs: e.matmul(
                            PS[0:np_, 2048 + h * dh:2048 + (h + 1) * dh], lhsT=PTs[0:kw, par * 8 + i * 2 + kh, 0:np_],
                            rhs=rhs, start=(kh == 0), stop=(kh == nkh - 1)),
                            r=[('PTs', par), rkey], w=[pk(4 + (h * dh) // 512)])
                if r == nr - 1:
                    FIN(s)

            def FIN(s):
                oi = s % 2
                O = PS[0:np_, 2048:3072].rearrange("p (h d) -> p h d", d=dh)
                P.op('dve', lambda e: e.tensor_tensor(
                    out=otok2[oi][0:np_, :].rearrange("p (h d) -> p h d", d=dh), in0=O,
                    in1=rinv2[0:np_, s % 2, 0:nh].unsqueeze(2).broadcast_to([np_, nh, dh]), op=ALU.mult),
                    r=[pk(4), pk(5), ('rinv', s % 2)], w=['otok%d' % oi])
                deferred.append(lambda: otok_T(oi, s, np_))

            S(0)
            SM(0)
            for idx in range(len(rounds)):
                if idx + 1 < len(rounds):
                    S(idx + 1)
                    SM(idx + 1)
                SM2(idx)
                TP(idx)
                pend = list(deferred)
                del deferred[:]
                if idx >= 1:
                    PV(idx - 1)
                for f_ in pend:
                    f_()
            PV(len(rounds) - 1)
            while deferred:
                deferred.pop(0)()

        ret_pending = []

        def ret_A(s, np_):
            c0 = s * 128
            for h in range(8):
                P.op('pe', lambda e, h=h: e.matmul(PS[0:np_, h * 128:h * 128 + np_], lhsT=krT[:, h, c0:c0 + np_],
                                                   rhs=QT[:, h, c0:c0 + np_], start=True, stop=True),
                     r=['krT', 'QT'], w=[pk(h // 4)])
            ai = s % 2
            AT = Pb[:, :, :].rearrange("p a b -> p (a b)")[:, ai * 1024:(ai + 1) * 1024].rearrange("p (h n) -> p h n", n=128)
            A_ps = PS[0:np_, 0:1024].rearrange("p (h n) -> p h n", n=128)[:, :, 0:np_]
            P.op('dve', lambda e: e.tensor_tensor(out=AT[0:np_, :, 0:np_], in0=A_ps,
                                                  in1=caus[0:np_, 0:np_].unsqueeze(1).broadcast_to([np_, 8, np_]),
                                                  op=ALU.mult), r=[pk(0), pk(1), 'caus'], w=[('AT', ai)])

        def ret_attend(s, np_, gC, gkey):
            c0 = s * 128
            ai = s % 2
            AT = Pb[:, :, :].rearrange("p a b -> p (a b)")[:, ai * 1024:(ai + 1) * 1024].rearrange("p (h n) -> p h n", n=128)
            for h in range(8):
                P.op('pe', lambda e, h=h: e.matmul(PS[0:np_, 2048 + h * 128:2048 + (h + 1) * 128],
                                                   lhsT=AT[0:np_, h, 0:np_], rhs=vr[0:np_, s, h * 128:(h + 1) * 128],
                                                   start=(h % 4 == 0), stop=False, skip_group_check=True),
                     r=[('AT', ai), 'vr'], w=[pk(4 + h // 4)])
            ret_dS(s, np_)
            for h in range(8):
                P.op('pe', lambda e, h=h: e.matmul(PS[0:np_, 2048 + h * 128:2048 + (h + 1) * 128],
                                                   lhsT=QT[:, h, c0:c0 + np_], rhs=Sbb[:, h, :],
                                                   start=False, stop=True, skip_group_check=True),
                     r=['QT', 'Sbb'], w=[pk(4 + h // 4)])
            ret_state_fin(s, np_, gC, gkey)
            O = PS[0:np_, 2048:3072].rearrange("p (h d) -> p h d", d=128)
            for h in range(8):
                P.op('act', lambda e, h=h: e.activation(out=qtok[0:np_, 0, :], in_=O[:, h, :], func=AF.Square,
                                                        accum_out=ssq8[0:np_, h:h + 1]),
                     r=[pk(4 + h // 4)], w=['qtok', 'ssq8'])
            P.op('act', lambda e: e.activation(out=std8[0:np_], in_=ssq8[0:np_], func=AF.Sqrt, scale=1.0 / 128, bias=EPS),
                 r=['ssq8'], w=['std8'])
            P.op('dve', lambda e: e.reciprocal(out=rstd8[0:np_], in_=std8[0:np_]), r=['std8'], w=['rstd8'])
            P.op('dve', lambda e: e.tensor_tensor(out=O, in0=O, in1=rstd8[0:np_].unsqueeze(2).broadcast_to([np_, 8, 128]),
                                                  op=ALU.mult), r=[pk(4), pk(5), 'rstd8'], w=[pk(4), pk(5)])
            oi = s % 2
            P.op('dve', lambda e: e.tensor_tensor(out=otok2[oi][0:np_, :], in0=PS[0:np_, 2048:3072], in1=gr[0:np_, s, :],
                                                  op=ALU.mult), r=[pk(4), pk(5), 'gr'], w=['otok%d' % oi])
            ret_pending.append(lambda: otok_T(oi, s, np_))

        def merge(b, nsub, nt):
            for cg in range(4):
                Wg, kg_ = loadw('win', 13 + b * 4 + cg)
                Wb_, kb_ = loadw('wbr', b * 4 + cg)
                for dd in range(4):
                    dc = cg * 4 + dd
                    gi = dd % 2
                    bg = fb()
                    for kc in range(KC):
                        P.op('pe', lambda e, kc=kc, dd=dd, bg=bg, Wg=Wg: e.matmul(
                            PS[:, bg * 512:bg * 512 + nt], lhsT=Wg[:, kc, dd * 128:(dd + 1) * 128],
                            rhs=uTb[:, kc, 0:nt], start=(kc == 0), stop=(kc == KC - 1)),
                            r=[kg_] + [('uT', s) for s in range(nsub)], w=[pk(bg)])
                    P.op('act', lambda e, gi=gi, dc=dc, bg=bg: e.activation(
                        out=gsb[gi][:, 0:nt], in_=PS[:, bg * 512:bg * 512 + nt], func=AF.Sigmoid,
                        bias=bgT[:, b * 16 + dc:b * 16 + dc + 1]), r=[pk(bg), 'bgT'], w=['gsb%d' % gi])
                    by = fb()
                    for kc in range(8):
                        P.op('pe', lambda e, kc=kc, dd=dd, by=by, Wb_=Wb_: e.matmul(
                            PS[:, by * 512:by * 512 + nt], lhsT=Wb_[:, kc, dd * 128:(dd + 1) * 128],
                            rhs=obT[:, kc, 0:nt], start=(kc == 0), stop=(kc == 7)),
                            r=[kb_, 'obT'], w=[pk(by)])
                    if b == 0:
                        P.op('dve', lambda e, gi=gi, dc=dc, by=by: e.tensor_tensor(
                            out=mT[:, dc, 0:nt], in0=PS[:, by * 512:by * 512 + nt], in1=gsb[gi][:, 0:nt], op=ALU.mult),
                            r=[pk(by), 'gsb%d' % gi], w=[('mT', dc)])
                    else:
                        P.op('dve', lambda e, gi=gi, by=by: e.tensor_tensor(
                            out=ytmp[gi][:, 0:nt], in0=PS[:, by * 512:by * 512 + nt], in1=gsb[gi][:, 0:nt], op=ALU.mult),
                            r=[pk(by), 'gsb%d' % gi], w=['ytmp%d' % gi])
                        P.op('pool', lambda e, gi=gi, dc=dc: e.tensor_tensor(
                            out=mT[:, dc, 0:nt], in0=mT[:, dc, 0:nt], in1=ytmp[gi][:, 0:nt], op=ALU.add),
                            r=['ytmp%d' % gi, ('mT', dc)], w=[('mT', dc)])

        import os as _os2
        CUT = int(_os2.environ.get('KCUT', '99'))
        CUT2 = int(_os2.environ.get('KCUT2', '99'))

        def full_tile(xsrc, cssrc, nsub, np_, first_mask=False, sample=False, do_down=True, y_dst=None,
                      k_out=None, v_out=None, prenormed=False, next_pre=None, xq='sp'):
            nt = nsub * np_ if not sample else np_
            subs = list(range(nsub))
            nk = 192 if sample else 256
            gC, gkey = (g64, 'g64') if sample else (g128, 'g128')
            fence(RETK + FFNK, XK + MIXK)
            if not prenormed:
                for s in subs:
                    load_x(s, np_, xsrc[s * 128:s * 128 + np_, :])
                    norm_T(s, np_, gmixT, 'gmixT')
            fence(XK, RETK)
            if CUT <= 1:
                return
            for t in range(2):
                Wq, kq_ = loadw('win', t)
                for a in range(4):
                    def ev(b, t=t, a=a):
                        P.op('act', lambda e: e.activation(out=QT[:, t * 4 + a, 0:nt], in_=PS[:, b * 512:b * 512 + nt],
                                                           func=AF.Copy, scale=0.125), r=[pk(b)], w=['QT'])
                    proj_b(Wq, kq_, a * 128, nsub, nt, ev)
            W2, k2 = loadw('win', 2)
            swa_kv_proj(W2, k2, subs, np_, nsub, nt, out_k=k_out, out_v=v_out, out_sub=nsub - 1)
            if CUT <= 2:
                return
            attn_all('swa', subs, np_, nk,
                     mask_for=lambda s_: (m0, 'm0') if (first_mask and s_ == 0) else (mstd, 'mstd'))
            if CUT <= 3:
                return
            merge(0, nsub, nt)
            if do_down and not sample:
                while conv_late:
                    conv_tile(*conv_late.pop(0))
            if CUT <= 4:
                return
            if not sample:
                P.op('pool', lambda e: e.tensor_copy(out=ksT[:, :, 0:128], in_=ksT[:, :, nt:nt + 128]),
                     r=['ksT'], w=['ksT'])
                P.op('pool', lambda e: e.tensor_copy(out=vstok[:, 0, :], in_=vstok[:, nsub, :]),
                     r=['vs%d' % nsub], w=['vs0'])
            for wt in (7, 8):
                Wv, kv_ = loadw('win', wt)
                for s in subs:
                    def ev(b, s=s, wt=wt):
                        P.op('act', lambda e: e.activation(out=vr[0:np_, s, (wt - 7) * 512:(wt - 6) * 512],
                                                           in_=PS[0:np_, b * 512:(b + 1) * 512], func=AF.Copy),
                             r=[pk(b)], w=['vr'])
                    proj_a(Wv, kv_, 0, 512, s, np_, ev)
            for wt in (9, 10):
                Wv, kv_ = loadw('win', wt)
                for s in subs:
                    def ev(b, s=s, wt=wt):
                        P.op('act', lambda e: e.activation(out=gr[0:np_, s, (wt - 9) * 512:(wt - 8) * 512],
                                                           in_=PS[0:np_, b * 512:(b + 1) * 512], func=AF.Silu),
                             r=[pk(b)], w=['gr'])
                    proj_a(Wv, kv_, 0, 512, s, np_, ev)
            cs_tabs = {}
            pending = []

            def flush(keep):
                while len(pending) > keep:
                    pending.pop(0)()
            for wt in (5, 6, 3, 4):
                Wv, kv_ = loadw('win', wt)
                isk = wt in (5, 6)
                hq = (wt - 5) if isk else (wt - 3)
                for s in subs:
                    if s not in cs_tabs or True:
                        cs_tabs[s] = load_cs(cssrc[s * 128:s * 128 + np_, :])
                    cs, cskey = cs_tabs[s]
                    def ev(b, s=s, isk=isk, hq=hq, cs=cs, cskey=cskey):
                        if isk:
                            dst = kt[0:np_, s, hq * 512:(hq + 1) * 512].rearrange("p (h d) -> p h d", d=128)
                            rotary(b, np_, cs, cskey, dk[0:np_, hq * 4:(hq + 1) * 4], 'dk', dst, ('kt', s, hq))
                            def later(s=s, hq=hq):
                                src = kt[:, s, hq * 512:(hq + 1) * 512]
                                t, tv = transpose_tok(src, ('kt', s, hq), np_, 4, None, None)
                                P.op('act', lambda e: e.activation(out=krT[:, hq * 4:(hq + 1) * 4, s * 128:s * 128 + np_],
                                                                   in_=tv[:, 0:4, 0:np_], func=AF.Copy),
                                     r=[pk(t)], w=['krT'])
                            pending.append(later)
                        else:
                            qi = rot['q'] % 4
                            rot['q'] += 1
                            rotary(b, np_, cs, cskey, dq[0:np_, hq * 4:(hq + 1) * 4], 'dq', qtok2[qi][0:np_], 'qtok%d' % qi)

                            def later(s=s, hq=hq, qi=qi):
                                src = qtok2[qi][:, :, :].rearrange("p a b -> p (a b)")
                                t, tv = transpose_tok(src, 'qtok%d' % qi, np_, 4, None, None)
                                P.op('act', lambda e: e.activation(out=QT[:, hq * 4:(hq + 1) * 4, s * 128:s * 128 + np_],
                                                                   in_=tv[:, 0:4, 0:np_], func=AF.Copy),
                                     r=[pk(t)], w=['QT'])
                            pending.append(later)
                    proj_a(Wv, kv_, 0, 512, s, np_, ev)
                    flush(3)
            flush(0)
            if CUT <= 5:
                return
            ret_A(0, np_)
            for s in subs:
                if s + 1 < nsub:
                    ret_A(s + 1, np_)
                ret_attend(s, np_, gC, gkey)
                while len(ret_pending) > 1:
                    ret_pending.pop(0)()
            while ret_pending:
                ret_pending.pop(0)()
            fence(RETK, XK)
            for s in subs:
                load_x(s, np_, xsrc[s * 128:s * 128 + np_, :], q=xq)
            if CUT <= 6:
                return
            merge(1, nsub, nt)
            if CUT <= 7:
                return
            for wt in (11, 12):
                Wq, kq_ = loadw('win', wt)
                for a in range(4):
                    def ev(b, wt=wt, a=a):
                        P.op('act', lambda e: e.activation(out=QT[:, (wt - 11) * 4 + a, 0:nt],
                                                           in_=PS[:, b * 512:b * 512 + nt],
                                                           func=AF.Copy, scale=1.0 / 16), r=[pk(b)], w=['QT'])
                    proj_b(Wq, kq_, a * 128, nsub, nt, ev)
            attn_all('mem', subs, np_, 256)
            merge(2, nsub, nt)
            if CUT <= 8:
                return
            for cg in range(4):
                Wo, ko_ = loadw('wo', cg)
                for s in subs:
                    b = fb()
                    for kc in range(KC):
                        P.op('pe', lambda e, kc=kc, s=s, b=b, Wo=Wo: e.matmul(
                            PS[0:np_, b * 512:(b + 1) * 512], lhsT=mT[:, kc, s * 128:s * 128 + np_],
                            rhs=Wo[:, kc, :], start=(kc == 0), stop=(kc == KC - 1)),
                            r=[ko_, ('mT', kc)], w=[pk(b)])
                    if cg == 3 and s >= 1:
                        T_part(s - 1, np_, gffnT, 'gffnT')
                    P.op('dve', lambda e, s=s, b=b, cg=cg: e.tensor_tensor(
                        out=Xb[0:np_, s, cg * 512:(cg + 1) * 512], in0=PS[0:np_, b * 512:(b + 1) * 512],
                        in1=Xb[0:np_, s, cg * 512:(cg + 1) * 512], op=ALU.add), r=[pk(b), ('X', s)], w=[('X', s)])
                    if cg == 3:
                        norm_part(s, np_, Xb[0:np_, s, :], ('X', s))
            if CUT <= 9:
                return
            T_part(nsub - 1, np_, gffnT, 'gffnT')
            fence(MIXK, FFNK)
            for i in range(11):
                Wg, kg_ = loadw('wup', i)
                Wv, kv_ = loadw('wup', 11 + i)
                if do_down and i == 2:
                    P.op('sp', lambda e: e.dma_start(out=gfin[:, :], in_=gfin_d.partition_broadcast(128)),
                         w=['gfin'], dma=True)
                for which, dd in ((0, 0), (0, 1), (1, 0), (1, 1), (0, 2), (0, 3), (1, 2), (1, 3)):
                    Wx, kx_ = (Wg, kg_) if which == 0 else (Wv, kv_)
                    j = i * 4 + dd
                    jb = j + 44 * which
                    ab = abuf[which][j % 2]
                    akey = 'abuf%d%d' % (which, j % 2)
                    ct = ctmp[which]
                    ckey = 'ctmp%d' % which
                    b = fb()
                    for kc in range(KC):
                        P.op('pe', lambda e, kc=kc, dd=dd, Wx=Wx, b=b: e.matmul(
                            PS[:, b * 512:b * 512 + nt], lhsT=Wx[:, kc, dd * 128:(dd + 1) * 128],
                            rhs=uTb[:, kc, 0:nt], start=(kc == 0), stop=(kc == KC - 1)),
                            r=[kx_] + [('uT', s) for s in subs], w=[pk(b)])
                    P.op('pool', lambda e, ab=ab, jb=jb: e.tensor_copy(out=ab[:, 0:2], in_=carry[:, jb, :]),
                         r=['carry'], w=[akey])
                    P.op('act', lambda e, ab=ab, b=b: e.activation(out=ab[:, 2:2 + nt], in_=PS[:, b * 512:b * 512 + nt],
                                                                   func=AF.Copy), r=[pk(b)], w=[akey])
                    P.op('pool', lambda e, ab=ab, jb=jb: e.tensor_copy(out=carry[:, jb, :], in_=ab[:, nt:nt + 2]),
                         r=[akey], w=['carry'])
                    P.op('dve', lambda e, ab=ab, jb=jb, ct=ct: e.tensor_scalar(
                        out=ct[:, 0:nt], in0=ab[:, 2:2 + nt], scalar1=wcT[:, jb, 2:3], scalar2=bcT[:, jb:jb + 1],
                        op0=ALU.mult, op1=ALU.add), r=[akey, 'wcT', 'bcT'], w=[ckey])
                    P.op('dve', lambda e, ab=ab, jb=jb, ct=ct: e.scalar_tensor_tensor(
                        out=ct[:, 0:nt], in0=ab[:, 1:1 + nt], scalar=wcT[:, jb, 1:2], in1=ct[:, 0:nt],
                        op0=ALU.mult, op1=ALU.add), r=[akey, 'wcT', ckey], w=[ckey])
                    P.op('dve', lambda e, ab=ab, jb=jb, ct=ct: e.scalar_tensor_tensor(
                        out=ct[:, 0:nt], in0=ab[:, 0:nt], scalar=wcT[:, jb, 0:1], in1=ct[:, 0:nt],
                        op0=ALU.mult, op1=ALU.add), r=[akey, 'wcT', ckey], w=[ckey])
                    if do_down:
                        sg = sgt2[dd % 2]
                        sgk = 'sgt%d' % (dd % 2)
                        if which == 0:
                            P.op('act', lambda e, sg=sg: e.activation(out=sg[:, 0:nt], in_=ctmp[0][:, 0:nt], func=AF.Silu),
                                 r=['ctmp0'], w=[sgk])
                        else:
                            P.op('dve', lambda e, j=j, sg=sg: e.tensor_tensor(out=hT[:, j, 0:nt], in0=sg[:, 0:nt],
                                                                             in1=ctmp[1][:, 0:nt], op=ALU.mult),
                                 r=[sgk, 'ctmp1'], w=[('hT', j)])
            if not do_down or CUT <= 10:
                return
            pre_sched = {}
            if next_pre is not None:
                nx_src, nx_nsub, nx_np = next_pre
                for s2 in range(nx_nsub):
                    pre_sched.setdefault(4 + s2, []).append(('norm', s2))
                    pre_sched.setdefault(5 + s2, []).insert(0, ('T', s2))
            blk = 0
            for cg in range(4):
                for kg in range(4):
                    Wd, kd_ = loadw('wdn', kg * 4 + cg)
                    for s in subs:
                        for k in range(11):
                            kc = kg * 11 + k
                            P.op('pe', lambda e, k=k, kc=kc, s=s, Wd=Wd: e.matmul(
                                PS[0:np_, s * 512:(s + 1) * 512], lhsT=hT[:, kc, s * 128:s * 128 + np_],
                                rhs=Wd[:, k, :], start=(kc == 0), stop=(kc == 43)),
                                r=[kd_, ('hT', kc)], w=[pk(s)])
                    for what, s2 in pre_sched.get(blk, []):
                        if what == 'norm':
                            P.op('sp', lambda e, s2=s2: e.dma_start(out=Xn[0:nx_np, :],
                                                                   in_=nx_src[s2 * 128:s2 * 128 + nx_np, :]),
                                 w=['Xn'], dma=True)
                            norm_part(s2, nx_np, Xn[0:nx_np, :], 'Xn')
                        else:
                            T_part(s2, nx_np, gmixT, 'gmixT')
                    blk += 1
                for s in subs:
                    P.op('dve', lambda e, s=s, cg=cg: e.tensor_tensor(
                        out=Xb[0:np_, s, cg * 512:(cg + 1) * 512], in0=PS[0:np_, s * 512:(s + 1) * 512],
                        in1=Xb[0:np_, s, cg * 512:(cg + 1) * 512], op=ALU.add), r=[pk(s), ('X', s)], w=[('X', s)])
            for s in subs:
                X = Xb[0:np_, s, :]
                P.op('act', lambda e, X=X: e.activation(out=Ub[0:np_, :], in_=X, func=AF.Square, accum_out=ss_[0:np_]),
                     r=[('X', s)], w=['U', 'ss'])
                P.op('act', lambda e: e.activation(out=std_[0:np_], in_=ss_[0:np_], func=AF.Sqrt, scale=1.0 / D, bias=EPS),
                     r=['ss'], w=['std'])
                P.op('dve', lambda e: e.reciprocal(out=rstd_[0:np_], in_=std_[0:np_]), r=['std'], w=['rstd'])
                P.op('dve', lambda e, X=X: e.scalar_tensor_tensor(out=X, in0=X, scalar=rstd_[0:np_, 0:1], in1=gfin[0:np_, :],
                                                                  op0=ALU.mult, op1=ALU.mult),
                     r=[('X', s), 'rstd', 'gfin'], w=[('X', s)])
                P.op('pool', lambda e, X=X, s=s: e.dma_start(out=y_dst[s * 128:s * 128 + np_, :], in_=X),
                     r=[('X', s)], dma=True)

        def mem_pass():
            fence(RETK + FFNK, XK + MIXK, eng='dve')
            pump(8)
            for s in range(2):
                load_x(s, 128, mem[s * 128:(s + 1) * 128, :])
                norm_T(s, 128, gmemT, 'gmemT')
                pump(2)
            for t in range(4):
                Wv, kv_ = loadw('mkv', t)
                if t < 2:
                    for a in range(4):
                        def ev(b, t=t, a=a):
                            P.op('act', lambda e: e.activation(out=mkT[:, t * 4 + a, :], in_=PS[:, b * 512:b * 512 + 256],
                                                               func=AF.Copy), r=[pk(b)], w=['mkT'])
                        proj_b(Wv, kv_, a * 128, 2, 256, ev)
                for s in range(2):
                    def ev(b, t=t, s=s):
                        i = s % 2
                        if t >= 2:
                            P.op('act', lambda e: e.activation(out=mvtok[:, s, (t - 2) * 512:(t - 1) * 512],
                                                               in_=PS[:, b * 512:(b + 1) * 512], func=AF.Copy),
                                 r=[pk(b)], w=['mvtok'])
                        P.op('dve', lambda e: e.tensor_copy(out=ctmp[i][:, :], in_=PS[:, b * 512:(b + 1) * 512]),
                             r=[pk(b)], w=['ctmp%d' % i])
                        dst = (mk_o if t < 2 else mv_o)[s * 128:(s + 1) * 128, (t % 2) * 512:(t % 2 + 1) * 512]
                        P.op('pool', lambda e: e.dma_start(out=dst, in_=ctmp[i][:, :]), r=['ctmp%d' % i], dma=True)
                    proj_a(Wv, kv_, 0, 512, s, 128, ev)

        def prefix_pass():
            NS = 31
            WRK = [('Wres', i) for i in range(4)]
            XSK = ['xrt0', 'xrt1', 'xrt2', 'xrt3', 'xqf']
            fence(MIXK + FFNK + ['Xn'], WRK + XSK, eng='dve')
            Wres = {}
            for i, wt in enumerate((7, 8, 5, 6)):
                conv_tile('win', wt)
                dstw = OV[:, i * 8192:(i + 1) * 8192]
                P.op('sp', lambda e, dstw=dstw, wt=wt: e.dma_start(out=dstw, in_=s_win[wt]), r=skeys('win', wt),
                     w=[('Wres', i)], dma=True)
                Wres[wt] = (dstw.rearrange("p (kc c) -> p kc c", c=512), ('Wres', i))
            xscr = ([Xn[:, i * 256:(i + 1) * 256].rearrange("p (a b) -> p a b", b=64) for i in range(4)],
                    Xn[:, 1024:1536].rearrange("p (h t d) -> p h t d", t=2, d=64), 'x')
            for t0 in range(0, NS, 4):
                subs = list(range(min(4, NS - t0)))
                fence(RETK, XK, eng='dve')
                for s in subs:
                    g = t0 + s
                    load_x(s, 128, xpre[g * 128:(g + 1) * 128, :])
                    norm_T(s, 128, gmixT, 'gmixT')
                    pump(5)
                fence(XK, RETK, eng='dve')
                nsub = len(subs)
                nt = nsub * 128
                for wt in (7, 8):
                    Wv, kv_ = Wres[wt]
                    for s in subs:
                        def ev(b, s=s, wt=wt):
                            P.op('act', lambda e: e.activation(out=vr[:, s, (wt - 7) * 512:(wt - 6) * 512],
                                                               in_=PS[:, b * 512:(b + 1) * 512], func=AF.Copy),
                                 r=[pk(b)], w=['vr'])
                        proj_a(Wv, kv_, 0, 512, s, 128, ev)
                for wt in (5, 6):
                    Wv, kv_ = Wres[wt]
                    hq = wt - 5
                    for s in subs:
                        g = t0 + s
                        cs, cskey = load_cs(cs_pre[g * 128:(g + 1) * 128, :])
                        def ev(b, s=s, hq=hq, cs=cs, cskey=cskey):
                            dst = kt[:, s, hq * 512:(hq + 1) * 512].rearrange("p (h d) -> p h d", d=128)
                            rotary(b, 128, cs, cskey, dk[:, hq * 4:(hq + 1) * 4], 'dk', dst, ('kt', s, hq), peng='dve', scr=xscr)
                        proj_a(Wv, kv_, 0, 512, s, 128, ev)
                if t0 + nsub == NS:
                    s = nsub - 1
                    W2, k2 = loadw('win', 2)
                    for blk in range(2):
                        b = fb()
                        for kc in range(KC):
                            P.op('pe', lambda e, kc=kc, b=b, blk=blk, s=s, W2=W2: e.matmul(
                                PS[:, b * 512:b * 512 + 128], lhsT=W2[:, kc, blk * 128:(blk + 1) * 128],
                                rhs=uTb[:, kc, s * 128:(s + 1) * 128], start=(kc == 0), stop=(kc == KC - 1)),
                                r=[k2, ('uT', s)], w=[pk(b)])
                        P.op('act', lambda e, b=b, blk=blk: e.activation(out=ksT[:, blk, 0:128], in_=PS[:, b * 512:b * 512 + 128],
                                                                         func=AF.Copy), r=[pk(b)], w=['ksT'])
                    def ev(b):
                        P.op('act', lambda e: e.activation(out=vstok[:, 0, :], in_=PS[:, b * 512:b * 512 + 256],
                                                           func=AF.Copy), r=[pk(b)], w=['vs0'])
                    proj_a(W2, k2, 256, 256, s, 128, ev)
                for s in subs:
                    ret_state_update(s, 128, g128, 'g128')
            fence(WRK + XSK, MIXK + FFNK + ['Xn'], eng='dve')

        def sample_setup():
            fence(RETK + FFNK, XK + MIXK)
            P.op('sp', lambda e: e.dma_start(out=Sf[:].rearrange("p a b -> p (a b)"), in_=sret), w=['Sf'], dma=True)
            P.op('act', lambda e: e.activation(out=Sbb[:], in_=Sf[:], func=AF.Copy), r=['Sf'], w=['Sbb'])
            P.op('sp', lambda e: e.dma_start(out=carry[:].rearrange("p a b -> p (a b)"), in_=sconv), w=['carry'], dma=True)
            P.op('sp', lambda e: e.dma_start(out=kvst[0][:, :], in_=ck), w=['kvst0'], dma=True)
            P.op('sp', lambda e: e.dma_start(out=kvst[1][:, :], in_=cv), w=['kvst1'], dma=True)
            P.op('dve', lambda e: e.tensor_copy(out=otok[:, 0:256], in_=kvst[0][:, :]), r=['kvst0'], w=['otok0'])
            P.op('dve', lambda e: e.tensor_copy(out=vstok[:, 0, :], in_=kvst[1][:, :]), r=['kvst1'], w=['vs0'])
            t, tv = transpose_tok(otok, 'otok0', 128, 2, None, None)
            P.op('act', lambda e: e.activation(out=ksT[:, :, 0:128], in_=tv[:, 0:2, :], func=AF.Copy), r=[pk(t)], w=['ksT'])
            P.op('pool', lambda e: e.dma_start(out=ks_o[0:64, :], in_=ck[64:128, :]), dma=True)
            P.op('pool', lambda e: e.dma_start(out=vs_o[0:64, :], in_=cv[64:128, :]), dma=True)
            for s in range(2):
                P.op('sp', lambda e, s=s: e.dma_start(out=Xb[:, s, 0:1024], in_=cmk[s * 128:(s + 1) * 128, :]),
                     w=[('X', s)], dma=True)
                P.op('sp', lambda e, s=s: e.dma_start(out=Xb[:, 2 + s, 0:1024], in_=cmv[s * 128:(s + 1) * 128, :]),
                     w=[('X', 2 + s)], dma=True)
                P.op('dve', lambda e, s=s: e.tensor_copy(out=mvtok[:, s, :], in_=Xb[:, 2 + s, 0:1024]),
                     r=[('X', 2 + s)], w=['mvtok'])
                P.op('dve', lambda e, s=s: e.tensor_copy(out=otok[:, :], in_=Xb[:, s, 0:1024]), r=[('X', s)], w=['otok0'])
                t, tv = transpose_tok(otok, 'otok0', 128, 8, None, None)
                P.op('act', lambda e, s=s, tv=tv: e.activation(out=mkT[:, :, s * 128:(s + 1) * 128], in_=tv[:, :, :],
                                                               func=AF.Copy), r=[pk(t)], w=['mkT'])

        import os as _os
        _st = _os.environ.get("KSTAGES", "mem,prefix,boundary,main,sample").split(",")
        _nmain = int(_os.environ.get("KNMAIN", "8"))
        if 'prefix' in _st:
            prefix_pass()
        if 'mem' in _st:
            mem_pass()
        if 'boundary' in _st:
            full_tile(xpre[3968:4096, :], cs_pre[3968:4096, :], 1, 128, do_down=False)
            P.op('dve', lambda e: e.tensor_scalar(out=carry[:].rearrange("p a b -> p (a b)"),
                                                  in0=carry[:].rearrange("p a b -> p (a b)"),
                                                  scalar1=hflag[:, 0:1], scalar2=None, op0=ALU.mult),
                 r=['carry', 'hflag'], w=['carry'])
        if 'main' in _st:
            for ti in range(_nmain):
                last = (ti == 7)
                if ti + 1 < _nmain:
                    nxt = (xp[(ti + 1) * 512:(ti + 2) * 512, :], 4, 128)
                elif 'sample' in _st:
                    nxt = (xs, 1, 64)
                else:
                    nxt = None
                full_tile(xp[ti * 512:(ti + 1) * 512, :], cs_main[ti * 512:(ti + 1) * 512, :], 4, 128,
                          first_mask=(ti == 0), y_dst=y_o[ti * 512:(ti + 1) * 512, :],
                          k_out=kp_o if last else None, v_out=vp_o if last else None,
                          prenormed=(ti > 0), next_pre=nxt, xq=('pool' if ti > 0 else 'sp'))
        P.op('pool', lambda e: e.dma_start(out=sp_o, in_=Sf[:].rearrange("p a b -> p (a b)")), r=['Sf'], dma=True)
        P.op('pool', lambda e: e.dma_start(out=cp_o, in_=carry[:].rearrange("p a b -> p (a b)")), r=['carry'], dma=True)
        if 'sample' in _st:
            sample_setup()
            full_tile(xs, cs_s, 1, 64, sample=True, y_dst=ys_o, k_out=ks_o[64:128, :], v_out=vs_o[64:128, :],
                      prenormed=('main' in _st))
        P.op('pool', lambda e: e.dma_start(out=ss_o, in_=Sf[:].rearrange("p a b -> p (a b)")), r=['Sf'], dma=True)
        P.op('pool', lambda e: e.dma_start(out=cs_o, in_=carry[:].rearrange("p a b -> p (a b)")), r=['carry'], dma=True)

        P.emit(nc, sems)
    return nc


_CACHE = {}


def _tables():
    half = 64
    inv_freq = (1.0 / (np.float32(10000.0) ** np.linspace(0.0, 1.0, half, dtype=np.float32))).astype(np.float32)

    def cs(pos):
        ang = pos.astype(np.float32)[:, None] * inv_freq[None, :]
        return np.concatenate([np.cos(ang), np.sin(ang)], axis=1).astype(np.float32)

    h = np.arange(8, dtype=np.float64)
    log_g = np.log1p(-np.exp2(-5.0 - h))
    n = np.arange(128, dtype=np.float64)
    dq = np.exp(log_g[None, :] * (n[:, None] + 1.0)).astype(np.float32)
    dk = (np.exp(-log_g[None, :] * (n[:, None] + 1.0)) * (128.0 ** -0.5)).astype(np.float32)
    g128 = np.exp(log_g * 128.0).astype(np.float32)
    g64 = np.exp(log_g * 64.0).astype(np.float32)
    m = np.arange(128)
    caus = (m[None, :] >= m[:, None]).astype(np.float32)
    sel = np.zeros((2, 128), np.float32)
    sel[0, :64] = 1.0
    sel[1, 64:] = 1.0
    mrow = np.zeros((2, 256), np.float32)
    mrow[0, 192:] = NEGM
    mrow[1, :64] = NEGM
    mstd = np.concatenate([mrow, mrow], axis=1)
    mrow0 = mrow.copy()
    mrow0[:, :128] = NEGM
    m0 = np.concatenate([mrow0, mrow0], axis=1)
    return cs, dq, dk, g128, g64, caus, sel, mstd, m0


def kernel(x_prompt, x_sample, mem_prompt, cache_swa_k, cache_swa_v, state_ret, state_ffn_conv,
           cache_mem_k, cache_mem_v, g_mix, w_in, b_gate, sink, w_br, w_o, g_mem, w_mem_kv,
           g_ffn, w_up, w_conv, b_conv, w_down, g_final):
    f = lambda a: np.ascontiguousarray(np.asarray(a, dtype=np.float32))
    x_prompt = f(x_prompt); x_sample = f(x_sample); mem_prompt = f(mem_prompt)
    if 'nc' not in _CACHE:
        _CACHE['nc'] = build_program()
    nc = _CACHE['nc']
    cs, dq, dk, g128, g64, caus, sel, mstd, m0 = _tables()

    def featT(v, nblk):
        return np.ascontiguousarray(f(v).reshape(nblk, 128).T)

    sk = f(sink)[0]
    sinkP = np.zeros(16, np.float32)
    sinkP[:] = sk
    wc = f(w_conv)[0]
    wcT = np.ascontiguousarray(wc.reshape(3, 88, 128).transpose(2, 1, 0)).reshape(128, 264)
    w_in_l = f(w_in)[0].copy()
    for t_ in range(2):
        blk_ = w_in_l[:, t_ * 512:(t_ + 1) * 512].reshape(D, 2, 4, 64)
        w_in_l[:, t_ * 512:(t_ + 1) * 512] = np.ascontiguousarray(blk_.transpose(0, 2, 1, 3)).reshape(D, 512)
    shared = {
        "w_in": w_in_l, "w_br": f(w_br)[0], "w_o": f(w_o)[0], "w_mkv": f(w_mem_kv)[0],
        "w_up": f(w_up)[0], "w_dn": f(w_down)[0],
        "gmixT": featT(f(g_mix)[0], 16), "gmemT": featT(f(g_mem)[0], 16), "gffnT": featT(f(g_ffn)[0], 16),
        "bgT": featT(f(b_gate)[0], 48), "wcT": wcT, "bcT": featT(f(b_conv)[0], 88),
        "sinkP": sinkP, "gfin": f(g_final),
        "cs_s": cs(1024 + np.arange(64)), "dq": dq, "dk": dk, "g128": g128, "g64": g64,
        "caus": caus, "sel": sel, "mstd": mstd,
    }
    zeros_pre = np.zeros((HALF, D), np.float32)
    in_maps = []
    for c in range(NCORE):
        b, half = c // 2, c % 2
        m = dict(shared)
        m["xp"] = x_prompt[b, half * HALF:(half + 1) * HALF]
        m["xpre"] = x_prompt[b, 0:HALF] if half == 1 else zeros_pre
        m["xs"] = x_sample[c]
        m["mem"] = mem_prompt[b]
        m["ck"] = f(cache_swa_k)[0, c].reshape(128, 256)
        m["cv"] = f(cache_swa_v)[0, c].reshape(128, 256)
        m["sret"] = np.ascontiguousarray(f(state_ret)[0, c].transpose(1, 0, 2)).reshape(128, 1024)
        m["sconv"] = np.ascontiguousarray(f(state_ffn_conv)[0, c].reshape(2, 88, 128).transpose(2, 1, 0)).reshape(128, 176)
        m["cmk"] = f(cache_mem_k)[0, c].reshape(256, 1024)
        m["cmv"] = f(cache_mem_v)[0, c].reshape(256, 1024)
        m["cs_pre"] = cs(np.arange(HALF))
        m["cs_main"] = cs(half * HALF + np.arange(HALF))
        m["m0"] = mstd if half == 1 else m0
        m["hflag"] = np.full((128, 1), float(half), np.float32)
        in_maps.append(m)
    if _CACHE.get('sim_hook') is not None:
        return _CACHE['sim_hook'](nc, in_maps)
    import os as _os3
    if _os3.environ.get('KONE'):
        c1 = int(_os3.environ['KONE'])
        return run_bass_kernel_spmd(nc, [in_maps[c1]], core_ids=[0]).results[0]
    ncr = int(_os3.environ.get('KCORES', NCORE))
    res = run_bass_kernel_spmd(nc, in_maps[:ncr], core_ids=list(range(ncr)))
    R = list(res.results) + [res.results[i % ncr] for i in range(ncr, NCORE)]
    y_prompt = np.stack([np.concatenate([R[2 * b]["y"], R[2 * b + 1]["y"]], axis=0) for b in range(4)])
    y_sample = np.stack([R[c]["ys"] for c in range(8)])
    kp = np.stack([R[2 * b + 1]["kp"].reshape(128, 4, 64) for b in range(4)])[None]
    vp = np.stack([R[2 * b + 1]["vp"].reshape(128, 4, 64) for b in range(4)])[None]
    sp = np.stack([R[2 * b + 1]["sp"].reshape(128, 8, 128).transpose(1, 0, 2) for b in range(4)])[None]
    cp = np.stack([R[2 * b + 1]["cp"].reshape(128, 88, 2).transpose(2, 1, 0).reshape(2, 11264) for b in range(4)])[None]
    mk = np.stack([R[2 * b]["mk"].reshape(256, 4, 256) for b in range(4)])[None]
    mv = np.stack([R[2 * b]["mv"].reshape(256, 4, 256) for b in range(4)])[None]
    ks = np.stack([R[c]["ks"].reshape(128, 4, 64) for c in range(8)])[None]
    vs = np.stack([R[c]["vs"].reshape(128, 4, 64) for c in range(8)])[None]
    ss = np.stack([R[c]["ss"].reshape(128, 8, 128).transpose(1, 0, 2) for c in range(8)])[None]
    cs_ = np.stack([R[c]["cs"].reshape(128, 88, 2).transpose(2, 1, 0).reshape(2, 11264) for c in range(8)])[None]
    outs = (y_prompt, y_sample, kp, vp, sp, cp, mk, mv, ks, vs, ss, cs_)
    return tuple(np.ascontiguousarray(o, dtype=np.float32) for o in outs)
```
